# Optimizing a Trainium2 kernel written in Bass

```python
import math
import jax, jax.numpy as jnp
from jax import lax
import numpy as np

D_MODEL = 2048
BATCH = 2
SEQ = 4096
DEPTH = 4

ATTN_HEADS = 8
ATTN_KV_HEADS = 2
HEAD_DIM = 128
WINDOW = 128
ROPE_THETA = 10000.0
DN_HEADS = 4
DN_HEAD_DIM = 128
DN_CONV = 4
DN_CHUNK = 64
S5_GROUPS = 32
S5_GROUP_CH = 16
S5_STATE = 64
ATTN_WIDTH = ATTN_HEADS * HEAD_DIM
ATTN_KV_WIDTH = ATTN_KV_HEADS * HEAD_DIM
DN_WIDTH = DN_HEADS * DN_HEAD_DIM
S5_WIDTH = S5_GROUPS * S5_GROUP_CH
MIX_WIDTH = ATTN_WIDTH + DN_WIDTH + S5_WIDTH
IN_SPLITS = (ATTN_WIDTH, ATTN_KV_WIDTH, ATTN_KV_WIDTH, 3 * DN_WIDTH, DN_WIDTH, DN_HEADS, DN_HEADS, S5_WIDTH)
IN_WIDTH = sum(IN_SPLITS)
D_FF = 5632
FFN_RES_WEIGHT = 0.5
NORM_EPS = 1e-6

kernel_name = "hymba_style_swa_deltanet_s5_macaron"


def rms_norm(x, gain):
    xf = x.astype(jnp.float32)
    y = xf * lax.rsqrt(jnp.mean(xf * xf, axis=-1, keepdims=True) + NORM_EPS)
    return (y * gain.astype(jnp.float32)).astype(x.dtype)


def l2_norm(x):
    return x * lax.rsqrt(jnp.sum(x * x, axis=-1, keepdims=True) + NORM_EPS)


def swiglu(h, w_gate, w_up, w_down):
    return (jax.nn.silu(h @ w_gate) * (h @ w_up)) @ w_down


def rope_tables(seq):
    half = HEAD_DIM // 2
    inv_freq = ROPE_THETA ** (-jnp.arange(half, dtype=jnp.float32) / half)
    ang = jnp.arange(seq, dtype=jnp.float32)[:, None] * inv_freq[None, :]
    return jnp.cos(ang), jnp.sin(ang)


def apply_rope(x, cos, sin):
    half = HEAD_DIM // 2
    xf = x.astype(jnp.float32)
    x1, x2 = xf[..., :half], xf[..., half:]
    c = cos[None, :, None, :]
    s = sin[None, :, None, :]
    return jnp.concatenate([x1 * c - x2 * s, x2 * c + x1 * s], axis=-1).astype(x.dtype)


def sliding_window_attention(q, k, v, sinks):
    b, s, hq, d = q.shape
    hkv = k.shape[2]
    grp = hq // hkv
    nb = s // WINDOW
    qb = q.reshape(b, nb, WINDOW, hkv, grp, d)
    kb = k.reshape(b, nb, WINDOW, hkv, d)
    vb = v.reshape(b, nb, WINDOW, hkv, d)
    pad = ((0, 0), (1, 0), (0, 0), (0, 0), (0, 0))
    kk = jnp.concatenate([jnp.pad(kb, pad)[:, :-1], kb], axis=2)
    vv = jnp.concatenate([jnp.pad(vb, pad)[:, :-1], vb], axis=2)
    scores = jnp.einsum('bnqhgd,bnkhd->bnhgqk', qb, kk, preferred_element_type=jnp.float32) * (d ** -0.5)
    qi = jnp.arange(WINDOW)[:, None] + WINDOW
    kj = jnp.arange(2 * WINDOW)[None, :]
    rel = qi - kj
    band = (rel >= 0) & (rel < WINDOW)
    first = (jnp.arange(nb) == 0)[:, None, None] & (kj < WINDOW)[None]
    mask = band[None] & jnp.logical_not(first)
    scores = jnp.where(mask[None, :, None, None], scores, -jnp.inf)
    sink = sinks.astype(jnp.float32).reshape(hkv, grp)[None, None, :, :, None, None]
    m = jnp.maximum(jnp.max(scores, axis=-1, keepdims=True), sink)
    p = jnp.exp(scores - m)
    p = p / (jnp.sum(p, axis=-1, keepdims=True) + jnp.exp(sink - m))
    out = jnp.einsum('bnhgqk,bnkhd->bnqhgd', p.astype(vv.dtype), vv)
    return out.reshape(b, s, hq * d)


def causal_depthwise_conv(u, w):
    taps = w.shape[0]
    return lax.conv_general_dilated(u, w[:, None, :], window_strides=(1,), padding=[(taps - 1, 0)],
                                    dimension_numbers=('NWC', 'WIO', 'NWC'), feature_group_count=u.shape[-1])


def gated_delta_rule(q, k, v, g, beta):
    b, s, h, dk = q.shape
    dv = v.shape[-1]
    c = DN_CHUNK
    n = s // c

    def chunks(t):
        return t.reshape(b, n, c, h, -1).transpose(0, 1, 3, 2, 4)

    q = chunks(q) * (dk ** -0.5)
    k = chunks(k)
    v = chunks(v)
    beta = chunks(beta[..., None])[..., 0]
    g = jnp.cumsum(chunks(g[..., None])[..., 0], axis=-1)
    causal = jnp.tril(jnp.ones((c, c), bool))
    strict = jnp.tril(jnp.ones((c, c), bool), -1)
    decay = jnp.exp(jnp.where(causal, g[..., :, None] - g[..., None, :], -jnp.inf))
    k_beta = k * beta[..., None]
    lower = jnp.where(strict, jnp.einsum('bnhid,bnhjd->bnhij', k_beta, k) * decay, 0.0) + jnp.eye(c, dtype=jnp.float32)
    rhs = jnp.concatenate([v * beta[..., None], k_beta * jnp.exp(g)[..., None]], axis=-1)
    uw = lax.linalg.triangular_solve(lower, rhs, left_side=True, lower=True, unit_diagonal=True)
    u, w = uw[..., :dv], uw[..., dv:]
    attn = jnp.where(causal, jnp.einsum('bnhid,bnhjd->bnhij', q, k) * decay, 0.0)
    q_dec = q * jnp.exp(g)[..., None]
    g_last = g[..., -1]
    k_dec = k * jnp.exp(g_last[..., None] - g)[..., None]

    def step(state, inp):
        q_c, k_c, u_c, w_c, a_c, gl = inp
        v_new = u_c - jnp.einsum('bhck,bhkv->bhcv', w_c, state)
        o = jnp.einsum('bhck,bhkv->bhcv', q_c, state) + jnp.einsum('bhij,bhjv->bhiv', a_c, v_new)
        state = state * jnp.exp(gl)[..., None, None] + jnp.einsum('bhck,bhcv->bhkv', k_c, v_new)
        return state, o

    mv = lambda t: jnp.moveaxis(t, 1, 0)
    s0 = jnp.zeros((b, h, dk, dv), jnp.float32)
    _, o = lax.scan(step, s0, (mv(q_dec), mv(k_dec), mv(u), mv(w), mv(attn), mv(g_last)))
    return o.transpose(1, 0, 3, 2, 4).reshape(b, s, h, dv)


def gated_deltanet(qkv_raw, z_gate, b_raw, a_raw, conv_w, a_log, dt_bias, norm_w):
    b, s, _ = qkv_raw.shape
    qkv = jax.nn.silu(causal_depthwise_conv(qkv_raw, conv_w)).astype(jnp.float32)
    q, k, v = jnp.split(qkv, 3, axis=-1)
    shp = (b, s, DN_HEADS, DN_HEAD_DIM)
    q = l2_norm(q.reshape(shp))
    k = l2_norm(k.reshape(shp))
    v = v.reshape(shp)
    beta = jax.nn.sigmoid(b_raw.astype(jnp.float32))
    g = -jnp.exp(a_log.astype(jnp.float32)) * jax.nn.softplus(a_raw.astype(jnp.float32) + dt_bias.astype(jnp.float32))
    o = gated_delta_rule(q, k, v, g, beta)
    o = o * lax.rsqrt(jnp.mean(o * o, axis=-1, keepdims=True) + NORM_EPS) * norm_w.astype(jnp.float32)
    o = o * jax.nn.silu(z_gate.astype(jnp.float32).reshape(shp))
    return o.reshape(b, s, DN_WIDTH).astype(qkv_raw.dtype)


def s5_mixer(u, a_re, a_im, log_dt, b_re, b_im, c_re, c_im, d_skip, glu_w, glu_b):
    b, s, _ = u.shape
    uf = u.astype(jnp.float32).reshape(b, s, S5_GROUPS, S5_GROUP_CH)
    lam = lax.complex(a_re.astype(jnp.float32), a_im.astype(jnp.float32))
    dt = jnp.exp(log_dt.astype(jnp.float32))[:, None]
    a_bar = jnp.exp(lam * dt)
    b_c = lax.complex(b_re.astype(jnp.float32), b_im.astype(jnp.float32))
    b_bar = ((a_bar - 1.0) / lam)[..., None] * b_c
    bu = jnp.einsum('bsgh,gph->bsgp', uf.astype(jnp.complex64), b_bar)
    a_seq = jnp.broadcast_to(a_bar, bu.shape)

    def combine(e1, e2):
        a1, x1 = e1
        a2, x2 = e2
        return a1 * a2, a2 * x1 + x2

    _, states = lax.associative_scan(combine, (a_seq, bu), axis=1)
    c_c = lax.complex(c_re.astype(jnp.float32), c_im.astype(jnp.float32))
    y = jnp.real(jnp.einsum('bsgp,ghp->bsgh', states, c_c))
    y = y + d_skip.astype(jnp.float32).reshape(S5_GROUPS, S5_GROUP_CH) * uf
    y = jax.nn.gelu(y.reshape(b, s, S5_WIDTH)).astype(u.dtype)
    return y * jax.nn.sigmoid(y @ glu_w + glu_b)


def setup_inputs(seed: int = 0) -> dict:
    key = jax.random.key(seed)
    ks = iter(jax.random.split(key, 40))
    f32 = jnp.float32

    def nrm(shape, scale):
        return scale * jax.random.normal(next(ks), shape, f32)

    def gain(width=D_MODEL):
        return 1.0 + nrm((DEPTH, width), 0.02)

    x = nrm((BATCH, SEQ, D_MODEL), 1.0)
    ff1_norm_pre = gain()
    ff1_w_gate = nrm((DEPTH, D_MODEL, D_FF), D_MODEL ** -0.5)
    ff1_w_up = nrm((DEPTH, D_MODEL, D_FF), D_MODEL ** -0.5)
    ff1_w_down = nrm((DEPTH, D_FF, D_MODEL), D_FF ** -0.5)
    ff1_norm_post = gain()
    mix_norm_pre = gain()
    w_in = nrm((DEPTH, D_MODEL, IN_WIDTH), D_MODEL ** -0.5)
    attn_sinks = nrm((DEPTH, ATTN_HEADS), 0.5)
    dn_conv_w = nrm((DEPTH, DN_CONV, 3 * DN_WIDTH), DN_CONV ** -0.5)
    dn_a_log = jnp.log(jax.random.uniform(next(ks), (DEPTH, DN_HEADS), f32, 1.0, 16.0))
    dn_dt = jnp.exp(jax.random.uniform(next(ks), (DEPTH, DN_HEADS), f32, math.log(1e-3), math.log(1e-1)))
    dn_dt_bias = dn_dt + jnp.log(-jnp.expm1(-dn_dt))
    dn_norm_w = gain(DN_HEAD_DIM)
    s5_a_re = -0.5 + nrm((DEPTH, S5_GROUPS, S5_STATE), 0.01)
    s5_a_im = math.pi * jnp.arange(S5_STATE, dtype=f32)[None, None, :] + nrm((DEPTH, S5_GROUPS, S5_STATE), 0.01)
    s5_log_dt = jax.random.uniform(next(ks), (DEPTH, S5_GROUPS), f32, math.log(1e-3), math.log(1e-1))
    s5_b_re = nrm((DEPTH, S5_GROUPS, S5_STATE, S5_GROUP_CH), (2 * S5_GROUP_CH) ** -0.5)
    s5_b_im = nrm((DEPTH, S5_GROUPS, S5_STATE, S5_GROUP_CH), (2 * S5_GROUP_CH) ** -0.5)
    s5_c_re = nrm((DEPTH, S5_GROUPS, S5_GROUP_CH, S5_STATE), (2 * S5_STATE) ** -0.5)
    s5_c_im = nrm((DEPTH, S5_GROUPS, S5_GROUP_CH, S5_STATE), (2 * S5_STATE) ** -0.5)
    s5_d = nrm((DEPTH, S5_WIDTH), 1.0)
    s5_glu_w = nrm((DEPTH, S5_WIDTH, S5_WIDTH), S5_WIDTH ** -0.5)
    s5_glu_b = nrm((DEPTH, S5_WIDTH), 0.01)
    w_out = nrm((DEPTH, MIX_WIDTH, D_MODEL), MIX_WIDTH ** -0.5)
    mix_norm_post = gain()
    ff2_norm_pre = gain()
    ff2_w_gate = nrm((DEPTH, D_MODEL, D_FF), D_MODEL ** -0.5)
    ff2_w_up = nrm((DEPTH, D_MODEL, D_FF), D_MODEL ** -0.5)
    ff2_w_down = nrm((DEPTH, D_FF, D_MODEL), D_FF ** -0.5)
    ff2_norm_post = gain()
    return {"x": x, "ff1_norm_pre": ff1_norm_pre, "ff1_w_gate": ff1_w_gate, "ff1_w_up": ff1_w_up,
            "ff1_w_down": ff1_w_down, "ff1_norm_post": ff1_norm_post, "mix_norm_pre": mix_norm_pre,
            "w_in": w_in, "attn_sinks": attn_sinks, "dn_conv_w": dn_conv_w, "dn_a_log": dn_a_log,
            "dn_dt_bias": dn_dt_bias, "dn_norm_w": dn_norm_w, "s5_a_re": s5_a_re, "s5_a_im": s5_a_im,
            "s5_log_dt": s5_log_dt, "s5_b_re": s5_b_re, "s5_b_im": s5_b_im, "s5_c_re": s5_c_re,
            "s5_c_im": s5_c_im, "s5_d": s5_d, "s5_glu_w": s5_glu_w, "s5_glu_b": s5_glu_b, "w_out": w_out,
            "mix_norm_post": mix_norm_post, "ff2_norm_pre": ff2_norm_pre, "ff2_w_gate": ff2_w_gate,
            "ff2_w_up": ff2_w_up, "ff2_w_down": ff2_w_down, "ff2_norm_post": ff2_norm_post}


def reference(x, ff1_norm_pre, ff1_w_gate, ff1_w_up, ff1_w_down, ff1_norm_post, mix_norm_pre, w_in,
              attn_sinks, dn_conv_w, dn_a_log, dn_dt_bias, dn_norm_w, s5_a_re, s5_a_im, s5_log_dt,
              s5_b_re, s5_b_im, s5_c_re, s5_c_im, s5_d, s5_glu_w, s5_glu_b, w_out, mix_norm_post,
              ff2_norm_pre, ff2_w_gate, ff2_w_up, ff2_w_down, ff2_norm_post):
    b, s, _ = x.shape
    cos, sin = rope_tables(s)
    offsets = np.cumsum(IN_SPLITS)[:-1].tolist()
    for l in range(DEPTH):
        h = rms_norm(x, ff1_norm_pre[l])
        x = x + FFN_RES_WEIGHT * rms_norm(swiglu(h, ff1_w_gate[l], ff1_w_up[l], ff1_w_down[l]), ff1_norm_post[l])
        h = rms_norm(x, mix_norm_pre[l])
        z = h @ w_in[l]
        aq, ak, av, dn_qkv, dn_z, dn_b, dn_a, s5_u = jnp.split(z, offsets, axis=-1)
        aq = apply_rope(aq.reshape(b, s, ATTN_HEADS, HEAD_DIM), cos, sin)
        ak = apply_rope(ak.reshape(b, s, ATTN_KV_HEADS, HEAD_DIM), cos, sin)
        av = av.reshape(b, s, ATTN_KV_HEADS, HEAD_DIM)
        y_attn = sliding_window_attention(aq, ak, av, attn_sinks[l])
        y_dn = gated_deltanet(dn_qkv, dn_z, dn_b, dn_a, dn_conv_w[l], dn_a_log[l],
                              dn_dt_bias[l], dn_norm_w[l])
        y_s5 = s5_mixer(s5_u, s5_a_re[l], s5_a_im[l], s5_log_dt[l], s5_b_re[l], s5_b_im[l],
                        s5_c_re[l], s5_c_im[l], s5_d[l], s5_glu_w[l], s5_glu_b[l])
        mixed = jnp.concatenate([y_attn, y_dn, y_s5], axis=-1) @ w_out[l]
        x = x + rms_norm(mixed, mix_norm_post[l])
        h = rms_norm(x, ff2_norm_pre[l])
        x = x + FFN_RES_WEIGHT * rms_norm(swiglu(h, ff2_w_gate[l], ff2_w_up[l], ff2_w_down[l]), ff2_norm_post[l])
    return x
```

```python
import os, math
import numpy as np
from contextlib import ExitStack
import ml_dtypes
import concourse.bass as bass
import concourse.mybir as mybir
from concourse.bass_utils import run_bass_kernel_spmd

F32 = mybir.dt.float32
BF16 = mybir.dt.bfloat16
AF = mybir.ActivationFunctionType
ALU = mybir.AluOpType

ENGS = ("pe", "act", "dve", "pool", "sp")
SEM_ROTATE = 30000
N_DMA_SLOTS = 40


class Prog:
    def __init__(self, nc, stack, same_engine_sync=True):
        self.nc = nc
        self.same_engine_sync = same_engine_sync
        lo, hi = nc._kernel_sem_range.start, nc._kernel_sem_range.stop
        nsem = hi - lo
        self.sems = [stack.enter_context(nc.semaphore(f"s{i}")) for i in range(98)]
        self.free = list(range(len(self.sems)))
        self.q = {e: [] for e in ENGS}
        self.cur = {}
        for e in ENGS:
            self.cur[e] = [self.free.pop(0), 0]
        self.dma_slots = {"sp": [[self.free.pop(0), 0, None] for _ in range(24)],
                          "pool": [[self.free.pop(0), 0, None] for _ in range(16)],
                          "act": [[self.free.pop(0), 0, None] for _ in range(4)]}
        self.dma_rr = {"sp": 0, "pool": 0, "act": 0}
        self.known = {e: {} for e in ENGS}
        self.sem_owner = {}
        for e in ENGS:
            self.sem_owner[self.cur[e][0]] = e
        self.lastw = {}
        self.readers = {}
        self.all_tokens = []
        self.n_inst = 0

    def _deps(self, eng, reads, writes):
        need = {}

        def add(tok):
            if tok is None:
                return
            s, v = tok
            if need.get(s, 0) < v:
                need[s] = v

        for b in reads:
            add(self.lastw.get(b))
        for b in writes:
            add(self.lastw.get(b))
            for t in self.readers.get(b, ()):
                add(t)
        out = []
        for s, v in need.items():
            if self.known[eng].get(s, 0) >= v:
                continue
            if self.sem_owner.get(s) == eng and (eng == "pe" or not self.same_engine_sync):
                continue
            self.known[eng][s] = v
            out.append((s, v))
        return out

    def _commit(self, tok, reads, writes):
        for b in writes:
            self.lastw[b] = tok
            self.readers[b] = []
        for b in reads:
            if b in writes:
                continue
            self.readers.setdefault(b, []).append(tok)

    def emit(self, eng, fn, reads=(), writes=()):
        waits = self._deps(eng, reads, writes)
        cur = self.cur[eng]
        if cur[1] >= SEM_ROTATE:
            s = self.free.pop(0)
            self.sem_owner[s] = eng
            cur[0], cur[1] = s, 0
        cur[1] += 1
        sem_i, val = cur[0], cur[1]
        sems = self.sems

        def run(engine):
            for s, v in waits:
                engine.wait_ge(sems[s], v)
            inst = fn(engine)
            inst.then_inc(sems[sem_i], 1)

        self.q[eng].append(run)
        tok = (sem_i, val)
        self._commit(tok, reads, writes)
        self.n_inst += 1
        return tok

    def dma(self, eng, fn, reads=(), writes=()):
        slots = self.dma_slots[eng]
        slot = slots[self.dma_rr[eng] % len(slots)]
        self.dma_rr[eng] += 1
        waits = self._deps(eng, reads, writes)
        if slot[1] > 0 and self.known[eng].get(slot[0], 0) < slot[1]:
            self.known[eng][slot[0]] = slot[1]
            waits.append((slot[0], slot[1]))
        slot[1] += 16
        sem_i, val = slot[0], slot[1]
        sems = self.sems

        def run(engine):
            for s, v in waits:
                engine.wait_ge(sems[s], v)
            inst = fn(engine)
            inst.then_inc(sems[sem_i], 16)

        self.q[eng].append(run)
        tok = (sem_i, val)
        self._commit(tok, reads, writes)
        self.n_inst += 1
        return tok

    def barrier(self, engines=ENGS):
        toks = [(self.cur[e][0], self.cur[e][1]) for e in ENGS if self.cur[e][1] > 0]
        toks += [(s[0], s[1]) for sl in self.dma_slots.values() for s in sl if s[1] > 0]
        sems = self.sems
        for e in engines:
            waits = []
            for s, v in toks:
                if self.sem_owner.get(s) == e and (e == "pe" or not self.same_engine_sync):
                    continue
                if self.known[e].get(s, 0) >= v:
                    continue
                self.known[e][s] = v
                waits.append((s, v))

            def run(engine, waits=waits):
                for s, v in waits:
                    engine.wait_ge(sems[s], v)

            self.q[e].append(run)
        self.lastw.clear()
        self.readers.clear()

    def finish(self, out_tokens):
        sems = self.sems
        waits = [(s, v) for (s, v) in out_tokens]

        def run(engine):
            for s, v in waits:
                engine.wait_ge(sems[s], v)

        self.q["sp"].append(run)

    def replay(self):
        nc = self.nc
        q = self.q
        with nc.Block() as block:

            @block.sync
            def _(e):
                for r in q["sp"]:
                    r(e)

            @block.tensor
            def _(e):
                for r in q["pe"]:
                    r(e)

            @block.scalar
            def _(e):
                for r in q["act"]:
                    r(e)

            @block.vector
            def _(e):
                for r in q["dve"]:
                    r(e)

            @block.gpsimd
            def _(e):
                for r in q["pool"]:
                    r(e)


EPS = 1e-6


class Ctx:
    pass


def norm_to_hT(C, P, src_tile, src_key, gain_bc, gain_key, hT, tt, xs, stage_keys):
    nc = C.nc
    D, KC = C.D, C.KC
    sq, ss, rstd, hb, ident, ps, psb = xs["sq"], xs["ss"], xs["rstd"], xs["hb"], xs["ident"], xs["ps"], xs["psb"]
    P.emit("act", lambda e: e.activation(out=sq[:, :], in_=src_tile, func=AF.Square, accum_out=ss[:, 0:1]),
           reads=[src_key], writes=["sq", "ss"])
    P.emit("act", lambda e: e.activation(out=ss[:, 1:2], in_=ss[:, 0:1], func=AF.Sqrt, scale=1.0 / D, bias=xs["epsb"][:, 0:1]),
           reads=["ss"], writes=["ss1"])
    P.emit("dve", lambda e: e.reciprocal(out=rstd[:, 0:1], in_=ss[:, 1:2]), reads=["ss1"], writes=["rstd"])
    P.emit("dve", lambda e: e.scalar_tensor_tensor(out=hb[:, :], in0=src_tile, scalar=rstd[:, 0:1], in1=gain_bc,
                                                   op0=ALU.mult, op1=ALU.mult),
           reads=[src_key, "rstd", gain_key], writes=["hb"])
    for g in range(0, KC, 8):
        n = min(8, KC - g)
        bank = C.tr_bank
        C.tr_bank = 6 + (C.tr_bank - 6 + 1) % 2

        def tr(e, g=g, n=n, bank=bank):
            inst = None
            for j in range(n):
                kc = g + j
                inst = e.transpose(out=psb[:, bank * 1024 + j * 128: bank * 1024 + (j + 1) * 128],
                                   in_=hb[:, kc * 128:(kc + 1) * 128], identity=ident[:, :])
            return inst

        P.emit("pe", tr, reads=["hb", "ident"], writes=[("ps", bank)])
        eng = "act" if (g // 8) % 2 == 0 else "dve"

        def cp(e, g=g, n=n, bank=bank, eng=eng):
            src = psb[:, bank * 1024: bank * 1024 + n * 128].rearrange("p (k t) -> p k t", k=n)
            dst = hT[:, g:g + n, tt * 128:(tt + 1) * 128]
            if eng == "act":
                return e.copy(out=dst, in_=src)
            return e.tensor_copy(out=dst, in_=src)

        P.emit(eng, cp, reads=[("ps", bank)], writes=[("hT", tt)])


def gate_up(C, P, hT, aT, wg_t, wu_t):
    nc = C.nc
    D, F, NT = C.D, C.F, C.NT
    KC, FC, TT = D // 128, F // 128, NT // 128
    HW = min(512, NT)
    NH = NT // HW
    ps = C.ps
    NWB = 3
    sub = ExitStack()
    sbs = lambda name, shape, dt: sub.enter_context(nc.sbuf_tensor("sb_" + name, shape, dt))
    wg = [sbs(f"wg{i}", [128, KC, 128], BF16) for i in range(NWB)]
    wu = [sbs(f"wu{i}", [128, KC, 128], BF16) for i in range(NWB)]
    sg = [sbs(f"sg{i}", [128, NT], F32) for i in range(2)]
    hkeys = [("hT", t) for t in range(TT)]
    for fc in range(FC):
        s = fc % NWB
        P.dma("pool", lambda e, fc=fc, s=s: e.dma_start(out=wg[s][:, :, :], in_=wg_t[fc]), reads=[], writes=[("wg", s)])
        P.dma("pool", lambda e, fc=fc, s=s: e.dma_start(out=wu[s][:, :, :], in_=wu_t[fc]), reads=[], writes=[("wu", s)])
        par = fc % 2
        for (w, wkey, boff) in ((wg, "wg", 0), (wu, "wu", 2)):
            for h in range(NH):
                bank = par * 4 + boff + h

                def mm(e, w=w, s=s, h=h, bank=bank):
                    inst = None
                    for kc in range(KC):
                        inst = e.matmul(ps[:, bank * 512: bank * 512 + HW], lhsT=w[s][:, kc, :],
                                        rhs=hT[:, kc, h * HW:(h + 1) * HW], start=(kc == 0), stop=(kc == KC - 1))
                    return inst

                P.emit("pe", mm, reads=[(wkey, s)] + hkeys, writes=[("ps", bank)])
        gb = par * 4
        ub = par * 4 + 2
        sgi = sg[par]
        P.emit("act", lambda e, gb=gb, sgi=sgi: e.activation(out=sgi[:, :], in_=ps[:, gb * 512: gb * 512 + NT], func=AF.Silu),
               reads=[("ps", gb + h) for h in range(NH)], writes=[("sg", par)])
        P.emit("dve", lambda e, ub=ub, sgi=sgi, fc=fc: e.tensor_tensor(out=aT[:, fc, :], in0=sgi[:, :], in1=ps[:, ub * 512: ub * 512 + NT], op=ALU.mult),
               reads=[("sg", par)] + [("ps", ub + h) for h in range(NH)], writes=[("aT", fc)])
    P.barrier()
    sub.close()


def down_post(C, P, aT, FC, wd_t, x_in, x_out, hT, hT_out_dram, gain_post, gain_next, ot_dram, res_w, ident_src):
    nc = C.nc
    D, NT = C.D, C.NT
    KC, TT = D // 128, NT // 128
    DC = D // 512
    FG = FC // 4
    ps = C.ps
    NDB = 3
    sub = ExitStack()
    sbs = lambda name, shape, dt: sub.enter_context(nc.sbuf_tensor("sb_" + name, shape, dt))
    wd = [sbs(f"wd{i}", [128, 4, 512], BF16) for i in range(NDB)]
    st = [sbs(f"st{i}", [128, 512], F32) for i in range(4)]
    sti = 0
    for dc in range(DC):
        for fg in range(FG):
            s = (dc * FG + fg) % NDB
            P.dma("pool", lambda e, dc=dc, fg=fg, s=s: e.dma_start(out=wd[s][:, :, :], in_=wd_t[dc, fg]), reads=[], writes=[("wd", s)])
            for fl in range(4):
                fc = fg * 4 + fl
                for tt in range(TT):
                    P.emit("pe", lambda e, s=s, fl=fl, fc=fc, tt=tt: e.matmul(
                        ps[:, tt * 512:(tt + 1) * 512], lhsT=aT[:, fc, tt * 128:(tt + 1) * 128], rhs=wd[s][:, fl, :],
                        start=(fc == 0), stop=(fc == FC - 1)),
                        reads=[("wd", s), ("aT", fc)], writes=[("ps", tt)])
        for tt in range(TT):
            k = sti % 4
            sti += 1
            if tt % 2 == 0:
                P.emit("act", lambda e, k=k, tt=tt: e.copy(out=st[k][:, :], in_=ps[:, tt * 512:(tt + 1) * 512]),
                       reads=[("ps", tt)], writes=[("st", k)])
            else:
                P.emit("dve", lambda e, k=k, tt=tt: e.tensor_copy(out=st[k][:, :], in_=ps[:, tt * 512:(tt + 1) * 512]),
                       reads=[("ps", tt)], writes=[("st", k)])
            P.dma("sp", lambda e, k=k, tt=tt, dc=dc: e.dma_start(out=ot_dram[tt * 128:(tt + 1) * 128, dc * 512:(dc + 1) * 512], in_=st[k][:, :]),
                  reads=[("st", k)], writes=[("otd", tt)])
    P.barrier()
    sub.close()
    sub = ExitStack()
    sb = lambda name, shape, dt: sub.enter_context(nc.sbuf_tensor("sb_" + name, shape, dt))
    xs = dict(C.xs)
    xs["sq"] = sb("sq", [128, D], BF16)
    xs["hb"] = sb("hb", [128, D], BF16)
    xs["ss"] = sb("ss", [128, 2], F32)
    xs["rstd"] = sb("rstd", [128, 1], F32)
    xs["ident"] = ident_src
    gpost = sb("gpost", [128, D], F32)
    P.dma("sp", lambda e: e.dma_start(out=gpost[:, :], in_=gain_post.partition_broadcast(128)), reads=[], writes=["gpost"])
    if gain_next is not None:
        gnext = sb("gnext", [128, D], F32)
        P.dma("sp", lambda e: e.dma_start(out=gnext[:, :], in_=gain_next.partition_broadcast(128)), reads=[], writes=["gnext"])
    ot = [sb(f"ot{i}", [128, D], F32) for i in range(2)]
    xt = [sb(f"xt{i}", [128, D], F32) for i in range(2)]
    out_toks = []
    for tt in range(TT):
        k = tt % 2
        P.dma("sp", lambda e, k=k, tt=tt: e.dma_start(out=ot[k][:, :], in_=ot_dram[tt * 128:(tt + 1) * 128, :]),
              reads=[("otd", tt)], writes=[("ot", k)])
        P.dma("sp", lambda e, k=k, tt=tt: e.dma_start(out=xt[k][:, :], in_=x_in[tt * 128:(tt + 1) * 128, :]),
              reads=[("xin", tt)], writes=[("xt", k)])
        sq, ss, rstd = xs["sq"], xs["ss"], xs["rstd"]
        P.emit("act", lambda e, k=k: e.activation(out=sq[:, :], in_=ot[k][:, :], func=AF.Square, accum_out=ss[:, 0:1]),
               reads=[("ot", k)], writes=["sq", "ss"])
        P.emit("act", lambda e: e.activation(out=ss[:, 1:2], in_=ss[:, 0:1], func=AF.Sqrt, scale=1.0 / D, bias=xs["epsb"][:, 0:1]),
               reads=["ss", "epsb"], writes=["ss1"])
        P.emit("dve", lambda e: e.reciprocal(out=rstd[:, 0:1], in_=ss[:, 1:2]), reads=["ss1"], writes=["rstd"])
        P.emit("dve", lambda e, k=k: e.scalar_tensor_tensor(out=ot[k][:, :], in0=ot[k][:, :], scalar=rstd[:, 0:1], in1=gpost[:, :],
                                                            op0=ALU.mult, op1=ALU.mult),
               reads=[("ot", k), "rstd", "gpost"], writes=[("ot", k)])
        P.emit("dve", lambda e, k=k: e.scalar_tensor_tensor(out=xt[k][:, :], in0=ot[k][:, :], scalar=float(res_w), in1=xt[k][:, :],
                                                            op0=ALU.mult, op1=ALU.add),
               reads=[("ot", k), ("xt", k)], writes=[("xt", k)])
        out_toks.append(P.dma("sp", lambda e, k=k, tt=tt: e.dma_start(out=x_out[tt * 128:(tt + 1) * 128, :], in_=xt[k][:, :]),
                              reads=[("xt", k)], writes=[("xout", tt)]))
        if gain_next is not None:
            norm_to_hT(C, P, xt[k][:, :], ("xt", k), gnext[:, :], "gnext", hT, tt, xs, None)
    if gain_next is not None and hT_out_dram is not None:
        out_toks.append(P.dma("sp", lambda e: e.dma_start(out=hT_out_dram.rearrange("(k p) t -> p k t", p=128), in_=hT[:, :, :]),
                              reads=[("hT", t) for t in range(TT)], writes=["hTout"]))
    P.barrier()
    sub.close()
    return out_toks


def ffn_phase(C, P, x_in, x_out, hT_in_dram, hT_out_dram, wg_t, wu_t, wd_t, gain_post, gain_next, ot_dram, ident_src):
    nc = C.nc
    D, F, NT = C.D, C.F, C.NT
    KC, FC, TT = D // 128, F // 128, NT // 128
    top = ExitStack()
    sb = lambda name, shape, dt: top.enter_context(nc.sbuf_tensor("sb_" + name, shape, dt))
    hT = sb("hT", [128, KC, NT], BF16)
    aT = sb("aT", [128, FC, NT], BF16)
    P.dma("sp", lambda e: e.dma_start(out=hT[:, :, :], in_=hT_in_dram.rearrange("(k p) t -> p k t", p=128)),
          reads=["hTout"], writes=[("hT", t) for t in range(TT)])
    gate_up(C, P, hT, aT, wg_t, wu_t)
    toks = down_post(C, P, aT, FC, wd_t, x_in, x_out, hT, hT_out_dram, gain_post, gain_next, ot_dram, 0.5, ident_src)
    top.close()
    return toks


def out_phase(C, P, x_in, x_out, yT_dram, hT_out_dram, wo_t, gluw_d, glub_d, gain_post, gain_next, ot_dram, ident_src):
    nc = C.nc
    D, NT = C.D, C.NT
    KC, TT = D // 128, NT // 128
    ps = C.ps
    top = ExitStack()
    sb = lambda name, shape, dt: top.enter_context(nc.sbuf_tensor("sb_" + name, shape, dt))
    hT = sb("hT", [128, KC, NT], BF16)
    yT = sb("yT", [128, 16, NT], BF16)
    sub = ExitStack()
    sbs = lambda name, shape, dt: sub.enter_context(nc.sbuf_tensor("sb_" + name, shape, dt))
    gw = sbs("gluw", [128, 4, 512], BF16)
    gb = sbs("glub", [128, 4], F32)
    gate = sbs("gate", [128, 4, NT], BF16)
    P.dma("sp", lambda e: e.dma_start(out=yT[:, :, :], in_=yT_dram.rearrange("(k p) t -> p k t", p=128)),
          reads=["yTd"], writes=[("aT", m) for m in range(16)])
    P.dma("pool", lambda e: e.dma_start(out=gw[:, :, :], in_=gluw_d), writes=["gluw"])
    P.dma("sp", lambda e: e.dma_start(out=gb[:, :], in_=glub_d), writes=["glub"])
    HW = min(512, NT)
    NH = NT // HW
    for cp in range(4):
        for h in range(NH):
            bank = (cp * NH + h) % 8

            def mm(e, cp=cp, h=h, bank=bank):
                inst = None
                for c in range(4):
                    inst = e.matmul(ps[:, bank * 512: bank * 512 + HW], lhsT=gw[:, c, cp * 128:(cp + 1) * 128], rhs=yT[:, 12 + c, h * HW:(h + 1) * HW],
                                    start=(c == 0), stop=(c == 3))
                return inst

            P.emit("pe", mm, reads=["gluw"] + [("aT", 12 + c) for c in range(4)], writes=[("ps", bank)])
            P.emit("act", lambda e, cp=cp, h=h, bank=bank: e.activation(out=gate[:, cp, h * HW:(h + 1) * HW], in_=ps[:, bank * 512: bank * 512 + HW],
                                                                         func=AF.Sigmoid, bias=gb[:, cp:cp + 1]),
                   reads=[("ps", bank), "glub"], writes=[("gate", cp)])
    for cp in range(4):
        P.emit("dve", lambda e, cp=cp: e.tensor_tensor(out=yT[:, 12 + cp, :], in0=yT[:, 12 + cp, :], in1=gate[:, cp, :], op=ALU.mult),
               reads=[("gate", cp), ("aT", 12 + cp)], writes=[("aT", 12 + cp)])
    P.barrier()
    sub.close()
    toks = down_post(C, P, yT, 16, wo_t, x_in, x_out, hT, hT_out_dram, gain_post, gain_next, ot_dram, 1.0, ident_src)
    top.close()
    return toks


def pre_phase(C, P, x_in, hT_out_dram, gain_next, ident_src):
    nc = C.nc
    D, NT = C.D, C.NT
    KC, TT = D // 128, NT // 128
    top = ExitStack()
    sb = lambda name, shape, dt: top.enter_context(nc.sbuf_tensor("sb_" + name, shape, dt))
    hT = sb("hT", [128, KC, NT], BF16)
    xs = dict(C.xs)
    xs["sq"] = sb("sq", [128, D], BF16)
    xs["hb"] = sb("hb", [128, D], BF16)
    xs["ss"] = sb("ss", [128, 2], F32)
    xs["rstd"] = sb("rstd", [128, 1], F32)
    xs["ident"] = ident_src
    gnext = sb("gnext", [128, D], F32)
    P.dma("sp", lambda e: e.dma_start(out=gnext[:, :], in_=gain_next.partition_broadcast(128)), reads=[], writes=["gnext"])
    xt = [sb(f"xt{i}", [128, D], F32) for i in range(2)]
    for tt in range(TT):
        k = tt % 2
        P.dma("sp", lambda e, k=k, tt=tt: e.dma_start(out=xt[k][:, :], in_=x_in[tt * 128:(tt + 1) * 128, :]), reads=[("xin", tt)], writes=[("xt", k)])
        norm_to_hT(C, P, xt[k][:, :], ("xt", k), gnext[:, :], "gnext", hT, tt, xs, None)
    toks = [P.dma("sp", lambda e: e.dma_start(out=hT_out_dram.rearrange("(k p) t -> p k t", p=128), in_=hT[:, :, :]),
                  reads=[("hT", t) for t in range(TT)], writes=["hTout"])]
    P.barrier()
    top.close()
    return toks


S = int(os.environ.get('MIX_S', '4096'))
NB = S // 128
NTB = S // 512
NEG = -30000.0
PENG = os.environ.get('PENG', 'pool')
PARTS = os.environ.get('MIX_PARTS', 'rdzta')

CO = {}
_off = 0
for _name, _w in (("ident", 128), ("ones", 128), ("rm", 128), ("mask_prev", 256), ("mask_cur", 256), ("tri", 128),
                  ("mpos", 128), ("mnegT", 128), ("strictT", 128)) + tuple((f"lvl{i}", 128) for i in range(7)):
    CO[_name] = (_off, _w)
    _off += _w
NCONST = _off
NCB = 896
NCF = 384


def make_consts():
    c = np.zeros((128, NCONST), np.float32)
    p = np.arange(128)[:, None]
    f = np.arange(128)[None, :]

    def put(name, arr):
        o, w = CO[name]
        c[:, o:o + w] = arr

    put("ident", (p == f).astype(np.float32))
    rm = np.zeros((128, 128), np.float32)
    for m in range(64):
        rm[m + 64, m] = -1.0
    for m in range(64, 128):
        rm[m - 64, m] = 1.0
    put("rm", rm)
    q = np.arange(128)[None, :]
    k = np.arange(128)[:, None]
    cur = np.where(k <= q, 0.0, NEG).astype(np.float32)
    prev = np.where(k > q, 0.0, NEG).astype(np.float32)
    put("mask_prev", np.concatenate([prev, prev], 1))
    put("mask_cur", np.concatenate([cur, cur], 1))
    put("tri", (p <= f).astype(np.float32))
    put("ones", np.ones((128, 128), np.float32))
    put("mpos", np.where(p < f, 1e4, 0.0))
    put("mnegT", np.where(f < p, -1e4, 0.0))
    put("strictT", (f > p).astype(np.float32))
    for lv in range(7):
        s = 1 << lv
        m = ((p // (2 * s)) == (f // (2 * s))) & ((p % (2 * s)) < s) & ((f % (2 * s)) >= s)
        put(f"lvl{lv}", m.astype(np.float32))
    return c


def rope_tables():
    half = 64
    inv_freq = (10000.0 ** (-np.arange(half, dtype=np.float32) / half)).astype(np.float32)
    ang = np.arange(S, dtype=np.float32)[:, None] * inv_freq[None, :]
    cos = np.cos(ang).astype(np.float32).T
    sin = np.sin(ang).astype(np.float32).T
    return (np.ascontiguousarray(np.concatenate([cos, cos], 0)), np.ascontiguousarray(np.concatenate([sin, sin], 0)))


def mixer_phase(C, P, stack, hT_full, win_fm_d, win_tm_d, consts_d, ropecos_d, ropesin_d, sinks_d, convw_d, dnp_d, dnnw_d,
                s5tab_d, s5b_d, s5c_d, s5d_d, yT_out, stages=("attn", "dn", "s5")):
    nc = C.nc
    D = C.D
    KC = D // 128
    ps = C.ps
    psb = C.xs["psb"]
    out_toks = []
    top = ExitStack()
    sbt = lambda name, shape, dt: top.enter_context(nc.sbuf_tensor("sb_" + name, shape, dt))
    cf = sbt("cf", [128, NCF], F32)
    cb = sbt("cb", [128, NCB], BF16)
    P.dma("sp", lambda e: e.dma_start(out=cf[:, :], in_=consts_d[:, 0:NCF]), writes=["cf"])
    P.dma("pool", lambda e: e.dma_start(out=cb[:, :], in_=consts_d[:, 0:NCB]), writes=["cb"])
    cfs = lambda name: cf[:, CO[name][0]:CO[name][0] + CO[name][1]]
    cbs = lambda name: cb[:, CO[name][0]:CO[name][0] + CO[name][1]]

    QT = sbt("QT", [128, NB, 2, 128], BF16)
    KT = sbt("KT", [128, S], BF16)
    Vtok = sbt("Vtok", [128, NB, 128], BF16)
    ba = sbt("ba", [128, NB, 8], F32)
    uT = sbt("uT", [128, S], BF16)
    szT = sbt("szT", [128, S], BF16)
    QdT = sbt("QdT", [128, S], F32)
    KdT = sbt("KdT", [128, S], F32)
    Vdtok = sbt("Vdtok", [128, NB, 128], F32)

    st1 = ExitStack()
    sb1 = lambda name, shape, dt: st1.enter_context(nc.sbuf_tensor("sb_" + name, shape, dt))
    wfm = sb1("wfm", [128, KC, 1024], BF16)
    wtm = sb1("wtm", [128, KC, 256], BF16)
    for g in range(4):
        P.dma("pool", lambda e, g=g: e.dma_start(out=wfm[:, g * 4:(g + 1) * 4, :], in_=win_fm_d[:, g * 4:(g + 1) * 4, :]), writes=[("wfm", g)])
    P.dma("pool", lambda e: e.dma_start(out=wtm[:, :, :], in_=win_tm_d), writes=["wtm"])
    convw = sb1("convw", [128, 3, 4], F32)
    P.dma("sp", lambda e: e.dma_start(out=convw[:, :, :], in_=convw_d), writes=["convw"])
    hTb = [sb1(f"hTb{i}", [128, KC, 512], BF16) for i in range(2)]
    rc = [sb1(f"rc{i}", [128, 512], F32) for i in range(1)] * 2
    rs = [sb1(f"rs{i}", [128, 512], F32) for i in range(1)] * 2
    zs = [sb1(f"zs{i}", [128, 512], F32) for i in range(2)]
    t1 = [sb1(f"t1{i}", [128, 512], F32) for i in range(1)] * 2
    t2 = [sb1(f"t2{i}", [128, 512], F32) for i in range(1)] * 2
    raw = [sb1(f"raw{c}", [128, 3 + 512], F32) for c in range(3)]
    cacc = [sb1(f"cacc{i}", [128, 512], F32) for i in range(2)]
    csil = [sb1(f"csil{i}", [128, 512], F32) for i in range(2)]
    csq = [sb1(f"csq{i}", [128, 512], F32) for i in range(1)] * 2
    crs = [sb1(f"crs{i}", [128, 512], F32) for i in range(1)] * 2
    for c in range(3):
        P.emit(PENG, lambda e, c=c: e.memset(raw[c][:, 0:3], 0.0), writes=[("raw", c)])
    wfm_keys = [("wfm", g) for g in range(4)]
    bank_rr = [0]

    def nbank():
        b = bank_rr[0]
        bank_rr[0] = (b + 1) % 6
        return b

    rr = [0]
    for tb in range(NTB):
        hb_ = hTb[tb % 2]
        hk = ("hTb", tb % 2)
        P.dma("sp", lambda e, tb=tb, hb_=hb_: e.dma_start(
            out=hb_[:, :, :], in_=hT_full[:, tb * 512:(tb + 1) * 512].rearrange("(k p) t -> p k t", p=128)), writes=[hk])
        k2 = 0
        P.dma("sp", lambda e, tb=tb, k2=k2: e.dma_start(out=rc[k2][:, :], in_=ropecos_d[:, tb * 512:(tb + 1) * 512]), writes=[("rc", k2)])
        P.dma("sp", lambda e, tb=tb, k2=k2: e.dma_start(out=rs[k2][:, :], in_=ropesin_d[:, tb * 512:(tb + 1) * 512]), writes=[("rs", k2)])

        def proj_fm(cc):
            bank = nbank()

            def mm(e, cc=cc, bank=bank, hb_=hb_):
                inst = None
                for kc in range(KC):
                    inst = e.matmul(ps[:, bank * 512:(bank + 1) * 512], lhsT=wfm[:, kc, cc * 128:(cc + 1) * 128],
                                    rhs=hb_[:, kc, :], start=(kc == 0), stop=(kc == KC - 1))
                return inst

            P.emit("pe", mm, reads=wfm_keys + [hk], writes=[("ps", bank)])
            return bank

        for cc in (range(3) if 'r' in PARTS else ()):
            bank = proj_fm(cc)
            i = rr[0] % 2
            rr[0] += 1
            P.emit("act", lambda e, i=i, bank=bank: e.copy(out=zs[i][:, :], in_=ps[:, bank * 512:(bank + 1) * 512]),
                   reads=[("ps", bank)], writes=[("zs", i)])
            rbank = nbank()
            P.emit("pe", lambda e, i=i, rbank=rbank: e.matmul(ps[:, rbank * 512:(rbank + 1) * 512], lhsT=cfs("rm"), rhs=zs[i][:, :],
                                                              start=True, stop=True),
                   reads=["cf", ("zs", i)], writes=[("ps", rbank)])
            P.emit(PENG, lambda e, i=i, k2=k2: e.tensor_tensor(out=t1[i][:, :], in0=zs[i][:, :], in1=rc[k2][:, :], op=ALU.mult),
                   reads=[("zs", i), ("rc", k2)], writes=[("t1", 0)])
            P.emit("dve", lambda e, i=i, k2=k2, rbank=rbank: e.tensor_tensor(out=t2[i][:, :], in0=ps[:, rbank * 512:(rbank + 1) * 512],
                                                                            in1=rs[k2][:, :], op=ALU.mult),
                   reads=[("ps", rbank), ("rs", k2)], writes=[("t2", 0)])
            if cc < 2:
                dst = QT[:, tb * 4:(tb + 1) * 4, cc, :]
                dkey = ("QT", tb)
                P.emit("dve", lambda e, i=i, dst=dst: e.tensor_tensor(out=dst, in0=t1[i][:, :].rearrange("p (n q) -> p n q", n=4),
                                                                      in1=t2[i][:, :].rearrange("p (n q) -> p n q", n=4), op=ALU.add),
                       reads=[("t1", 0), ("t2", 0)], writes=[dkey])
            else:
                P.emit("dve", lambda e, i=i, tb=tb: e.tensor_tensor(out=KT[:, tb * 512:(tb + 1) * 512], in0=t1[i][:, :], in1=t2[i][:, :], op=ALU.add),
                       reads=[("t1", 0), ("t2", 0)], writes=[("KT", tb)])
        for c in (range(3) if 'd' in PARTS else ()):
            bank = proj_fm(3 + c)
            P.emit("act", lambda e, c=c, bank=bank: e.copy(out=raw[c][:, 3:515], in_=ps[:, bank * 512:(bank + 1) * 512]),
                   reads=[("ps", bank)], writes=[("raw", c)])
            i = rr[0] % 2
            rr[0] += 1
            eng = PENG
            P.emit(eng, lambda e, c=c, i=i: e.tensor_scalar(out=cacc[i][:, :], in0=raw[c][:, 0:512], scalar1=convw[:, c, 0:1], scalar2=None, op0=ALU.mult),
                   reads=[("raw", c), "convw"], writes=[("cacc", i)])
            for k in range(1, 4):
                P.emit("dve", lambda e, c=c, i=i, k=k: e.scalar_tensor_tensor(out=cacc[i][:, :], in0=raw[c][:, k:k + 512], scalar=convw[:, c, k:k + 1],
                                                                              in1=cacc[i][:, :], op0=ALU.mult, op1=ALU.add),
                       reads=[("raw", c), "convw", ("cacc", i)], writes=[("cacc", i)])
            P.emit(PENG, lambda e, c=c: e.tensor_copy(out=raw[c][:, 0:3], in_=raw[c][:, 512:515]), reads=[("raw", c)], writes=[("raw", c)])
            P.emit("act", lambda e, i=i: e.activation(out=csil[i][:, :], in_=cacc[i][:, :], func=AF.Silu),
                   reads=[("cacc", i)], writes=[("csil", i)])
            if c < 2:
                P.emit(PENG, lambda e, i=i: e.tensor_tensor(out=csq[i][:, :], in0=csil[i][:, :], in1=csil[i][:, :], op=ALU.mult),
                       reads=[("csil", i)], writes=[("csq", 0)])
                sbank = nbank()
                P.emit("pe", lambda e, i=i, sbank=sbank: e.matmul(ps[:, sbank * 512:(sbank + 1) * 512], lhsT=cfs("ones"), rhs=csq[i][:, :], start=True, stop=True),
                       reads=["cf", ("csq", 0)], writes=[("ps", sbank)])
                P.emit("act", lambda e, i=i, sbank=sbank: e.activation(out=crs[i][:, :], in_=ps[:, sbank * 512:(sbank + 1) * 512], func=AF.Ln,
                                                                       bias=C.xs["epsb"][:, 0:1], scale=1.0),
                       reads=[("ps", sbank), "epsb"], writes=[("crs", 0)])
                P.emit("act", lambda e, i=i: e.activation(out=crs[i][:, :], in_=crs[i][:, :], func=AF.Exp, scale=-0.5),
                       reads=[("crs", 0)], writes=[("crs", 0)])
                dstT = QdT if c == 0 else KdT
                sc = (128.0 ** -0.5) if c == 0 else 1.0
                P.emit("dve", lambda e, i=i, tb=tb, dstT=dstT, sc=sc: e.scalar_tensor_tensor(
                    out=dstT[:, tb * 512:(tb + 1) * 512], in0=csil[i][:, :], scalar=sc, in1=crs[i][:, :], op0=ALU.mult, op1=ALU.mult),
                    reads=[("csil", i), ("crs", 0)], writes=[("QdT" if c == 0 else "KdT", tb)])
            else:
                tbank = nbank()

                def trv(e, i=i, tbank=tbank):
                    inst = None
                    for t in range(4):
                        inst = e.transpose(out=ps[:, tbank * 512 + t * 128: tbank * 512 + (t + 1) * 128], in_=csil[i][:, t * 128:(t + 1) * 128],
                                           identity=cfs("ident"))
                    return inst

                P.emit("pe", trv, reads=["cf", ("csil", i)], writes=[("ps", tbank)])
                P.emit("act", lambda e, tb=tb, tbank=tbank: e.copy(out=Vdtok[:, tb * 4:(tb + 1) * 4, :],
                                                                  in_=ps[:, tbank * 512:(tbank + 1) * 512].rearrange("p (n d) -> p n d", n=4)),
                       reads=[("ps", tbank)], writes=[("Vdtok", tb)])
        bank = proj_fm(6)
        if 'z' in PARTS: P.emit("act", lambda e, tb=tb, bank=bank: e.activation(out=szT[:, tb * 512:(tb + 1) * 512], in_=ps[:, bank * 512:(bank + 1) * 512], func=AF.Silu),
               reads=[("ps", bank)], writes=[("szT", tb)])
        bank = proj_fm(7)
        P.emit("dve", lambda e, tb=tb, bank=bank: e.tensor_copy(out=uT[:, tb * 512:(tb + 1) * 512], in_=ps[:, bank * 512:(bank + 1) * 512]),
               reads=[("ps", bank)], writes=[("uT", tb)])
        for t in (range(4) if 't' in PARTS else ()):
            n = tb * 4 + t
            bank = nbank()

            def mmt(e, t=t, bank=bank, hb_=hb_):
                inst = None
                for kc in range(KC):
                    inst = e.matmul(ps[:, bank * 512: bank * 512 + 256], lhsT=hb_[:, kc, t * 128:(t + 1) * 128], rhs=wtm[:, kc, :],
                                    start=(kc == 0), stop=(kc == KC - 1))
                return inst

            P.emit("pe", mmt, reads=["wtm", hk], writes=[("ps", bank)])
            P.emit("act", lambda e, n=n, bank=bank: e.copy(out=Vtok[:, n, :], in_=ps[:, bank * 512: bank * 512 + 128]),
                   reads=[("ps", bank)], writes=[("Vtok", n)])
            P.emit("act", lambda e, n=n, bank=bank: e.copy(out=ba[:, n, :], in_=ps[:, bank * 512 + 128: bank * 512 + 136]),
                   reads=[("ps", bank)], writes=["ba"])
    P.barrier()
    st1.close()

    if "attn" in stages and "a" in PARTS:
        st2 = ExitStack()
        sb2 = lambda name, shape, dt: st2.enter_context(nc.sbuf_tensor("sb_" + name, shape, dt))
        sk = sb2("sk", [128, 2], F32)
        sinkb = sb2("sinkb", [128, 256], F32)
        P.dma("sp", lambda e: e.dma_start(out=sk[:, :], in_=sinks_d.partition_broadcast(128)), writes=["sk"])
        P.emit("act", lambda e: e.activation(out=sk[:, :], in_=sk[:, :], func=AF.Exp), reads=["sk"], writes=["sk"])
        for h in range(2):
            P.emit("dve", lambda e, h=h: e.tensor_scalar(out=sinkb[:, h * 128:(h + 1) * 128], in0=cfs("ones"), scalar1=sk[:, h:h + 1], scalar2=None, op0=ALU.mult),
                   reads=["cf", "sk"], writes=["sinkb"])
        PT = [sb2(f"PT{i}", [128, 512], BF16) for i in range(2)]
        den = [sb2(f"den{i}", [128, 256], F32) for i in range(2)]
        yst = [sb2(f"yst{i}", [128, 256], BF16) for i in range(2)]
        scale = 128.0 ** -0.5
        for n in range(NB):
            i = n % 2
            sbank = n % 2
            obank = 2 + n % 2
            kbs = [(n - 1, "mask_prev"), (n, "mask_cur")] if n > 0 else [(n, "mask_cur")]

            def sc_mm(e, n=n, sbank=sbank, kbs=kbs):
                inst = None
                for j, (kb, mname) in enumerate(kbs):
                    o = ps[:, sbank * 512 + j * 256: sbank * 512 + (j + 1) * 256]
                    e.matmul(o, lhsT=KT[:, kb * 128:(kb + 1) * 128], rhs=QT[:, n, :, :].rearrange("p h q -> p (h q)"), start=True, stop=False)
                    inst = e.matmul(o, lhsT=cbs("ident"), rhs=cbs(mname), start=False, stop=True)
                return inst

            P.emit("pe", sc_mm, reads=[("KT", max(n - 1, 0) // 4), ("KT", n // 4), ("QT", n // 4), "cb"], writes=[("ps", sbank)])
            w = 256 * len(kbs)
            P.emit("act", lambda e, i=i, sbank=sbank, w=w: e.activation(out=PT[i][:, 0:w], in_=ps[:, sbank * 512: sbank * 512 + w], func=AF.Exp, scale=scale),
                   reads=[("ps", sbank)], writes=[("PT", i)])

            def o_mm(e, n=n, i=i, obank=obank, kbs=kbs):
                inst = None
                nk = len(kbs)
                for j, (kb, _) in enumerate(kbs):
                    inst = e.matmul(ps[:, obank * 512: obank * 512 + 256], lhsT=Vtok[:, kb, :], rhs=PT[i][:, j * 256:(j + 1) * 256],
                                    start=(j == 0), stop=(j == nk - 1))
                for j, (kb, _) in enumerate(kbs):
                    inst = e.matmul(ps[:, obank * 512 + 256: obank * 512 + 512], lhsT=cbs("ones"), rhs=PT[i][:, j * 256:(j + 1) * 256],
                                    start=(j == 0), stop=(j == nk - 1))
                return inst

            P.emit("pe", o_mm, reads=[("Vtok", max(n - 1, 0)), ("Vtok", n), ("PT", i), "cb"], writes=[("ps", obank)])
            P.emit("dve", lambda e, i=i, obank=obank: e.tensor_tensor(out=den[i][:, :], in0=ps[:, obank * 512 + 256: obank * 512 + 512], in1=sinkb[:, :], op=ALU.add),
                   reads=[("ps", obank), "sinkb"], writes=[("den", i)])
            P.emit("dve", lambda e, i=i: e.reciprocal(out=den[i][:, :], in_=den[i][:, :]), reads=[("den", i)], writes=[("den", i)])
            P.emit("dve", lambda e, i=i, obank=obank: e.tensor_tensor(out=yst[i][:, :], in0=ps[:, obank * 512: obank * 512 + 256], in1=den[i][:, :], op=ALU.mult),
                   reads=[("ps", obank), ("den", i)], writes=[("yst", i)])
            out_toks.append(P.dma("sp", lambda e, n=n, i=i: e.dma_start(
                out=yT_out[0:256, n * 128:(n + 1) * 128].rearrange("(h d) q -> d h q", h=2), in_=yst[i][:, :].rearrange("p (h q) -> p h q", h=2)),
                reads=[("yst", i)], writes=[("yout_a", n)]))
        P.barrier()
        st2.close()

    M = dict(ba=ba, uT=uT, szT=szT, QdT=QdT, KdT=KdT, Vdtok=Vdtok, cf=cf, cb=cb, cfs=cfs, cbs=cbs)
    if "dn" in stages:
        st3 = ExitStack()
        for _ in dn_stage(C, P, nc, M, dnp_d, dnnw_d, consts_d, yT_out, out_toks, st3):
            pass
        P.barrier()
        st3.close()
    if "s5" in stages:
        st4 = ExitStack()
        for _ in s5_stage(C, P, nc, M, s5tab_d, s5b_d, s5c_d, s5d_d, yT_out, out_toks, st4):
            pass
        P.barrier()
        st4.close()
    return out_toks, top


def dn_stage(C, P, nc, M, dnp_d, dnnw_d, consts_d, yT_out, out_toks, st):
    ps = C.ps
    psb = C.xs["psb"]
    sb = lambda name, shape, dt: st.enter_context(nc.sbuf_tensor("sb_" + name, shape, dt))
    ba, szT, QdT, KdT, Vdtok = M["ba"], M["szT"], M["QdT"], M["KdT"], M["Vdtok"]
    cfs = M["cfs"]
    NCD = NCONST - NCB
    cd = sb("cd", [128, NCD], F32)
    P.dma("sp", lambda e: e.dma_start(out=cd[:, :], in_=consts_d[:, NCB:NCONST]), writes=["cd"])
    cds = lambda name: cd[:, CO[name][0] - NCB: CO[name][0] - NCB + CO[name][1]]
    dnp = sb("dnp", [128, 2], F32)
    nw = sb("nw", [128, 1], F32)
    P.dma("sp", lambda e: e.dma_start(out=dnp[:, :], in_=dnp_d.partition_broadcast(128)), writes=["dnp"])
    P.dma("sp", lambda e: e.dma_start(out=nw[:, :], in_=dnnw_d), writes=["nw"])
    oneb = sb("oneb", [128, 1], F32)
    P.emit("dve", lambda e: e.memset(oneb[:, :], 1.0), writes=["oneb"])
    beta = sb("beta", [128, NB], F32)
    gall = sb("gall", [128, NB], F32)
    gc = sb("gc", [128, NB], F32)
    negegc = sb("negegc", [128, NB], F32)
    egl = sb("egl", [128, NB], F32)
    kds = sb("kds", [128, NB], F32)
    tmp = sb("dtmp", [128, NB], F32)
    nA = sb("nA", [128, 1], F32)
    P.emit("act", lambda e: e.activation(out=beta[:, :], in_=ba[:, :, 0], func=AF.Sigmoid), reads=["ba"], writes=["beta"])
    P.emit("act", lambda e: e.activation(out=nA[:, :], in_=dnp[:, 0:1], func=AF.Exp), reads=["dnp"], writes=["nA"])
    P.emit("dve", lambda e: e.tensor_scalar(out=nA[:, :], in0=nA[:, :], scalar1=-1.0, scalar2=None, op0=ALU.mult), reads=["nA"], writes=["nA"])
    P.emit("act", lambda e: e.activation(out=tmp[:, :], in_=ba[:, :, 1], func=AF.Exp, bias=dnp[:, 1:2]), reads=["ba", "dnp"], writes=["dtmp"])
    P.emit("act", lambda e: e.activation(out=tmp[:, :], in_=tmp[:, :], func=AF.Ln, bias=oneb[:, 0:1]), reads=["dtmp", "oneb"], writes=["dtmp"])
    P.emit("dve", lambda e: e.tensor_scalar(out=gall[:, :], in0=tmp[:, :], scalar1=nA[:, 0:1], scalar2=None, op0=ALU.mult), reads=["dtmp", "nA"], writes=["gall"])
    P.emit("pe", lambda e: e.matmul(ps[:, 6 * 512: 6 * 512 + NB], lhsT=cds("tri"), rhs=gall[:, :], start=True, stop=True),
           reads=["cd", "gall"], writes=[("ps", 6)])
    P.emit("dve", lambda e: e.tensor_copy(out=gc[:, :], in_=ps[:, 6 * 512: 6 * 512 + NB]), reads=[("ps", 6)], writes=["gc"])
    P.emit("pe", lambda e: e.matmul(ps[:, 6 * 512: 6 * 512 + NB], lhsT=cfs("ones"), rhs=gall[:, :], start=True, stop=True),
           reads=["cf", "gall"], writes=[("ps", 6)])
    P.emit("dve", lambda e: e.tensor_copy(out=tmp[:, :], in_=ps[:, 6 * 512: 6 * 512 + NB]), reads=[("ps", 6)], writes=["dtmp"])
    P.emit("act", lambda e: e.activation(out=egl[:, :], in_=tmp[:, :], func=AF.Exp), reads=["dtmp"], writes=["egl"])
    P.emit("dve", lambda e: e.tensor_tensor(out=kds[:, :], in0=tmp[:, :], in1=gc[:, :], op=ALU.subtract), reads=["dtmp", "gc"], writes=["kds"])
    P.emit("act", lambda e: e.activation(out=kds[:, :], in_=kds[:, :], func=AF.Exp), reads=["kds"], writes=["kds"])
    P.emit("act", lambda e: e.activation(out=negegc[:, :], in_=gc[:, :], func=AF.Exp), reads=["gc"], writes=["negegc"])
    P.emit("dve", lambda e: e.tensor_scalar(out=negegc[:, :], in0=negegc[:, :], scalar1=-1.0, scalar2=None, op0=ALU.mult), reads=["negegc"], writes=["negegc"])
    yield

    NSL = 4
    Gb = [sb(f"Gb{i}", [128, 128], F32) for i in range(2)]
    Rs = [sb(f"Rs{i}", [128, 128], F32) for i in range(2)]
    XT = [sb(f"XT{i}", [128, 128], F32) for i in range(2)]
    DmT = [sb(f"DmT{i}", [128, 128], F32) for i in range(2)]
    DmsT = [sb(f"DmsT{i}", [128, 128], F32) for i in range(2)]
    ER = [sb(f"ER{i}", [128, 128], F32) for i in range(2)]
    U = [sb(f"U{i}", [128, 128], F32) for i in range(2)]
    Bl = [sb(f"Bl{i}", [128, 7, 128], F32) for i in range(2)]
    YT = [sb(f"YT{i}", [128, 128], F32) for i in range(2)]
    NN = [sb(f"NN{i}", [128, 256], F32) for i in range(NSL)]
    AT = [sb(f"AT{i}", [128, 128], F32) for i in range(NSL)]
    Kdec = [sb(f"Kdec{i}", [128, 128], F32) for i in range(NSL)]
    Qdec = [sb(f"Qdec{i}", [128, 128], F32) for i in range(NSL)]
    Sst = [sb(f"Sst{i}", [128, 128], F32) for i in range(2)]
    yv = sb("yv", [128, 128], F32)
    vnew = [sb(f"vnew{i}", [128, 128], F32) for i in range(2)]
    dss = sb("dss", [128, 2], F32)
    drs = sb("drs", [128, 1], F32)
    onb = [sb(f"onb{i}", [128, 128], BF16) for i in range(2)]
    ysd = [sb(f"ysd{i}", [128, 128], BF16) for i in range(2)]
    identb = M["cbs"]("ident")
    P.emit("dve", lambda e: e.memset(Sst[0][:, :], 0.0), writes=[("Sst", 0)])

    def prep_pair(n0):
        tiles = [n0, n0 + 1]
        for k, n in enumerate(tiles):
            sl = slice(n * 128, (n + 1) * 128)
            slot = n % NSL
            P.emit("dve", lambda e, k=k, n=n: e.tensor_scalar(out=Gb[k][:, :], in0=cfs("ones"), scalar1=gall[:, n:n + 1], scalar2=None, op0=ALU.mult),
                   reads=["cf", "gall"], writes=[("Gb", k)])
            P.emit("pe", lambda e, k=k: e.matmul(ps[:, k * 128:(k + 1) * 128], lhsT=Gb[k][:, :], rhs=cds("tri"), start=True, stop=True),
                   reads=[("Gb", k), "cd"], writes=[("ps", 0)])
            P.emit("act", lambda e, k=k: e.copy(out=Rs[k][:, :], in_=ps[:, k * 128:(k + 1) * 128]), reads=[("ps", 0)], writes=[("Rs", k)])
            P.emit("dve", lambda e, k=k, n=n: e.scalar_tensor_tensor(out=XT[k][:, :], in0=Rs[k][:, :], scalar=gc[:, n:n + 1], in1=cds("mnegT"),
                                                                      op0=ALU.subtract, op1=ALU.add),
                   reads=[("Rs", k), "gc", "cd"], writes=[("XT", k)])
            P.emit("act", lambda e, k=k: e.activation(out=DmT[k][:, :], in_=XT[k][:, :], func=AF.Exp), reads=[("XT", k)], writes=[("DmT", k)])
            P.emit("act", lambda e, k=k: e.activation(out=ER[k][:, :], in_=Rs[k][:, :], func=AF.Exp), reads=[("Rs", k)], writes=[("ER", k)])
            P.emit("pool", lambda e, k=k: e.tensor_tensor(out=DmsT[k][:, :], in0=DmT[k][:, :], in1=cds("strictT"), op=ALU.mult),
                   reads=[("DmT", k), "cd"], writes=[("DmsT", k)])
            P.emit("pool", lambda e, k=k, slot=slot, sl=sl: e.tensor_tensor(out=Qdec[slot][:, :], in0=QdT[:, sl], in1=ER[k][:, :], op=ALU.mult),
                   reads=[("ER", k), ("QdT", n // 4)], writes=[("Qdec", slot)])

            def kkq(e, k=k, sl=sl):
                e.matmul(ps[:, 512 + k * 256: 512 + k * 256 + 128], lhsT=KdT[:, sl], rhs=KdT[:, sl], start=True, stop=True)
                return e.matmul(ps[:, 512 + k * 256 + 128: 512 + k * 256 + 256], lhsT=KdT[:, sl], rhs=QdT[:, sl], start=True, stop=True)

            P.emit("pe", kkq, reads=[("KdT", n // 4), ("QdT", n // 4)], writes=[("ps", 1)])
            P.emit("dve", lambda e, k=k, n=n: e.scalar_tensor_tensor(out=U[k][:, :], in0=ps[:, 512 + k * 256: 512 + k * 256 + 128], scalar=beta[:, n:n + 1],
                                                                      in1=DmsT[k][:, :], op0=ALU.mult, op1=ALU.mult),
                   reads=[("ps", 1), "beta", ("DmsT", k)], writes=[("U", k)])
            P.emit("dve", lambda e, k=k, slot=slot: e.tensor_tensor(out=AT[slot][:, :], in0=ps[:, 512 + k * 256 + 128: 512 + k * 256 + 256], in1=DmT[k][:, :], op=ALU.mult),
                   reads=[("ps", 1), ("DmT", k)], writes=[("AT", slot)])
            P.emit("pe", lambda e, k=k, sl=sl: e.transpose(out=ps[:, 256 + k * 128: 256 + (k + 1) * 128], in_=KdT[:, sl], identity=cfs("ident")),
                   reads=[("KdT", n // 4), "cf"], writes=[("ps", 0)])
            P.emit("act", lambda e, k=k, n=n, slot=slot: e.activation(out=Kdec[slot][:, :], in_=ps[:, 256 + k * 128: 256 + (k + 1) * 128], func=AF.Copy,
                                                                        scale=kds[:, n:n + 1]),
                   reads=[("ps", 0), "kds"], writes=[("Kdec", slot)])
            for lv in range(7):
                P.emit("pool", lambda e, k=k, lv=lv: e.tensor_tensor(out=Bl[k][:, lv, :], in0=U[k][:, :], in1=cds(f"lvl{lv}"), op=ALU.mult),
                       reads=[("U", k), "cd"], writes=[("Bl", k, lv)])
            P.emit("dve", lambda e, slot=slot: e.tensor_copy(out=NN[slot][:, 0:128], in_=cfs("ident")), reads=["cf"], writes=[("NN", slot)])
            P.emit("dve", lambda e, slot=slot: e.tensor_copy(out=NN[slot][:, 128:256], in_=cfs("ident")), reads=["cf"], writes=[("NN", slot)])
        yield
        for lv in range(7):
            last = lv == 6
            for k, n in enumerate(tiles):
                slot = n % NSL
                P.emit("pe", lambda e, k=k, lv=lv, slot=slot: e.matmul(ps[:, (2 + k) * 512: (2 + k) * 512 + 128], lhsT=Bl[k][:, lv, :], rhs=NN[slot][:, 128:256],
                                                                        start=True, stop=True),
                       reads=[("Bl", k, lv), ("NN", slot)], writes=[("ps", 2 + k)])
                P.emit("act", lambda e, k=k: e.copy(out=YT[k][:, :], in_=ps[:, (2 + k) * 512: (2 + k) * 512 + 128]), reads=[("ps", 2 + k)], writes=[("YT", k)])
            for k, n in enumerate(tiles):
                slot = n % NSL

                def zz(e, k=k, slot=slot, last=last):
                    inst = e.matmul(ps[:, (4 + k) * 512: (4 + k) * 512 + 128], lhsT=YT[k][:, :], rhs=NN[slot][:, 0:128], start=True, stop=True)
                    if not last:
                        inst = e.matmul(ps[:, (4 + k) * 512 + 128: (4 + k) * 512 + 256], lhsT=NN[slot][:, 0:128], rhs=YT[k][:, :], start=True, stop=True)
                    return inst

                P.emit("pe", zz, reads=[("YT", k), ("NN", slot)], writes=[("ps", 4 + k)])
                w = 128 if last else 256
                P.emit("dve", lambda e, k=k, slot=slot, w=w: e.tensor_tensor(out=NN[slot][:, 0:w], in0=NN[slot][:, 0:w], in1=ps[:, (4 + k) * 512: (4 + k) * 512 + w],
                                                                               op=ALU.subtract),
                       reads=[("NN", slot), ("ps", 4 + k)], writes=[("NN", slot)])
            yield

    def serial(n):
        sl = slice(n * 128, (n + 1) * 128)
        slot = n % NSL
        So, Sn = Sst[n % 2], Sst[(n + 1) % 2]
        ko, kn = ("Sst", n % 2), ("Sst", (n + 1) % 2)
        cb0 = 6 * 512
        P.emit("pe", lambda e: e.matmul(ps[:, cb0: cb0 + 128], lhsT=KdT[:, sl], rhs=So[:, :], start=True, stop=True),
               reads=[("KdT", n // 4), ko], writes=[("ps", 6)])
        P.emit("dve", lambda e: e.scalar_tensor_tensor(out=yv[:, :], in0=ps[:, cb0: cb0 + 128], scalar=negegc[:, n:n + 1], in1=Vdtok[:, n, :],
                                                       op0=ALU.mult, op1=ALU.add),
               reads=[("ps", 6), "negegc", ("Vdtok", n // 4)], writes=["yv"])
        yield
        P.emit("pe", lambda e: e.matmul(ps[:, cb0 + 128: cb0 + 256], lhsT=NN[slot][:, 0:128], rhs=yv[:, :], start=True, stop=True),
               reads=[("NN", slot), "yv"], writes=[("ps", 6)])
        vn = vnew[n % 2]
        vk = ("vnew", n % 2)
        P.emit("dve", lambda e: e.tensor_scalar(out=vn[:, :], in0=ps[:, cb0 + 128: cb0 + 256], scalar1=beta[:, n:n + 1], scalar2=None, op0=ALU.mult),
               reads=[("ps", 6), "beta"], writes=[vk])
        yield
        P.emit("pe", lambda e: e.matmul(ps[:, cb0 + 256: cb0 + 384], lhsT=Kdec[slot][:, :], rhs=vn[:, :], start=True, stop=True),
               reads=[("Kdec", slot), vk], writes=[("ps", 6)])
        P.emit("dve", lambda e: e.scalar_tensor_tensor(out=Sn[:, :], in0=So[:, :], scalar=egl[:, n:n + 1], in1=ps[:, cb0 + 256: cb0 + 384],
                                                       op0=ALU.mult, op1=ALU.add),
               reads=[ko, "egl", ("ps", 6)], writes=[kn])
        yield
        ob0 = 7 * 512

        def omm(e):
            e.matmul(ps[:, ob0: ob0 + 128], lhsT=Qdec[slot][:, :], rhs=So[:, :], start=True, stop=False)
            return e.matmul(ps[:, ob0: ob0 + 128], lhsT=AT[slot][:, :], rhs=vn[:, :], start=False, stop=True)

        P.emit("pe", omm, reads=[("Qdec", slot), ko, ("AT", slot), vk], writes=[("ps", 7)])
        P.emit("act", lambda e: e.activation(out=XT[0][:, :], in_=ps[:, ob0: ob0 + 128], func=AF.Square, accum_out=dss[:, 0:1]),
               reads=[("ps", 7)], writes=[("XT", 0), "dss"])
        P.emit("act", lambda e: e.activation(out=dss[:, 1:2], in_=dss[:, 0:1], func=AF.Sqrt, scale=1.0 / 128, bias=C.xs["epsb"][:, 0:1]),
               reads=["dss", "epsb"], writes=["dss1"])
        P.emit("dve", lambda e: e.reciprocal(out=drs[:, :], in_=dss[:, 1:2]), reads=["dss1"], writes=["drs"])
        i2 = n % 2
        P.emit("act", lambda e: e.activation(out=onb[i2][:, :], in_=ps[:, ob0: ob0 + 128], func=AF.Copy, scale=drs[:, 0:1]),
               reads=[("ps", 7), "drs"], writes=[("onb", i2)])
        tb0 = (6 * 512 + 384) * 2
        P.emit("pe", lambda e: e.transpose(out=psb[:, tb0: tb0 + 128], in_=onb[i2][:, :], identity=identb[:, :]),
               reads=[("onb", i2), "cb"], writes=[("ps", 6)])
        P.emit("dve", lambda e: e.scalar_tensor_tensor(out=ysd[i2][:, :], in0=psb[:, tb0: tb0 + 128], scalar=nw[:, 0:1], in1=szT[:, sl],
                                                       op0=ALU.mult, op1=ALU.mult),
               reads=[("ps", 6), "nw", ("szT", n // 4)], writes=[("ysd", i2)])
        out_toks.append(P.dma("sp", lambda e: e.dma_start(out=yT_out[256:384, sl], in_=ysd[i2][:, :]), reads=[("ysd", i2)], writes=[("yout_d", n)]))
        yield

    def chain_pair(n0):
        for n in (n0, n0 + 1):
            yield from serial(n)

    yield from prep_pair(0)
    for n0 in range(0, NB, 2):
        gens = [chain_pair(n0)]
        if n0 + 2 < NB:
            gens.append(prep_pair(n0 + 2))
        while gens:
            for g in list(gens):
                try:
                    next(g)
                except StopIteration:
                    gens.remove(g)
            yield


def s5_stage(C, P, nc, M, s5tab_d, s5b_d, s5c_d, s5d_d, yT_out, out_toks, st):
    ps = C.ps
    sb = lambda name, shape, dt: st.enter_context(nc.sbuf_tensor("sb_" + name, shape, dt))
    uT = M["uT"]
    L = int(os.environ.get('S5L', min(512, S)))
    NCH = S // L
    LG = L.bit_length() - 1
    TWO_PI = 2.0 * math.pi
    tab = sb("s5tab", [128, 4, 3], F32)
    bblk = sb("s5bb", [128, 4, 2, 128], BF16)
    cblk = sb("s5cb", [128, 4, 2, 128], BF16)
    dcol = sb("s5dc", [128, 1], F32)
    P.dma("sp", lambda e: e.dma_start(out=tab[:, :, :], in_=s5tab_d), writes=["s5tab"])
    P.dma("pool", lambda e: e.dma_start(out=bblk[:, :, :, :], in_=s5b_d), writes=["s5bb"])
    P.dma("pool", lambda e: e.dma_start(out=cblk[:, :, :, :], in_=s5c_d), writes=["s5cb"])
    P.dma("sp", lambda e: e.dma_start(out=dcol[:, :], in_=s5d_d), writes=["s5dc"])
    P.emit("dve", lambda e: e.tensor_scalar(out=cblk[:, :, 1, :], in0=cblk[:, :, 1, :], scalar1=-1.0, scalar2=None, op0=ALU.mult), reads=["s5cb"], writes=["s5cb"])
    sm = {}
    for nm in ("dt", "mag", "u", "uc", "ti", "tf", "sn", "cs", "x", "y", "den", "cr", "ci", "t0", "t1", "t2"):
        sm[nm] = sb("s5_" + nm, [128, 4], F32)
    smi = sb("s5_int", [128, 4], mybir.dt.int32)
    zb = sb("s5zb", [128, 1], F32)
    P.emit("dve", lambda e: e.memset(zb[:, :], 0.0), writes=["s5zb"])
    k = ["s5small"]

    def small(eng, fn):
        P.emit(eng, fn, reads=k + ["s5tab", "s5zb"], writes=k)

    a_re, a_im, ldt = tab[:, :, 0], tab[:, :, 1], tab[:, :, 2]

    def horner(dst, x, coefs):
        small("dve", lambda e: e.memset(sm[dst][:, :], float(coefs[0])))
        for cf_ in coefs[1:]:
            small("dve", lambda e: e.tensor_tensor(out=sm[dst][:, :], in0=sm[dst][:, :], in1=sm[x][:, :], op=ALU.mult))
            small("dve", lambda e, cf_=cf_: e.tensor_scalar(out=sm[dst][:, :], in0=sm[dst][:, :], scalar1=float(cf_), scalar2=None, op0=ALU.add))

    fact = [1.0]
    for i_ in range(1, 20):
        fact.append(fact[-1] * i_)
    small("dve", lambda e: e.tensor_scalar(out=sm["t0"][:, :], in0=ldt, scalar1=1.0 / 16, scalar2=None, op0=ALU.mult))
    horner("dt", "t0", [1.0 / fact[k_] for k_ in range(9, -1, -1)])
    for _ in range(4):
        small("dve", lambda e: e.tensor_tensor(out=sm["dt"][:, :], in0=sm["dt"][:, :], in1=sm["dt"][:, :], op=ALU.mult))
    small("dve", lambda e: e.tensor_tensor(out=sm["t0"][:, :], in0=a_re, in1=sm["dt"][:, :], op=ALU.mult))
    horner("mag", "t0", [1.0 / fact[k_] for k_ in range(7, -1, -1)])
    small("dve", lambda e: e.tensor_tensor(out=sm["u"][:, :], in0=a_im, in1=sm["dt"][:, :], op=ALU.mult))
    small("dve", lambda e: e.tensor_scalar(out=sm["u"][:, :], in0=sm["u"][:, :], scalar1=1.0 / TWO_PI, scalar2=None, op0=ALU.mult))
    small("dve", lambda e: e.tensor_scalar(out=sm["uc"][:, :], in0=sm["u"][:, :], scalar1=0.25, scalar2=None, op0=ALU.add))

    def reduce_sin(src, dst):
        small("dve", lambda e: e.tensor_copy(out=smi[:, :], in_=sm[src][:, :]))
        small("dve", lambda e: e.tensor_copy(out=sm["tf"][:, :], in_=smi[:, :]))
        small("dve", lambda e: e.tensor_tensor(out=sm["ti"][:, :], in0=sm[src][:, :], in1=sm["tf"][:, :], op=ALU.subtract))
        small("dve", lambda e: e.tensor_scalar(out=sm["t1"][:, :], in0=sm["ti"][:, :], scalar1=0.5, scalar2=None, op0=ALU.is_gt))
        small("dve", lambda e: e.tensor_scalar(out=sm["t2"][:, :], in0=sm["ti"][:, :], scalar1=-0.5, scalar2=None, op0=ALU.is_lt))
        small("dve", lambda e: e.tensor_tensor(out=sm["ti"][:, :], in0=sm["ti"][:, :], in1=sm["t1"][:, :], op=ALU.subtract))
        small("dve", lambda e: e.tensor_tensor(out=sm["ti"][:, :], in0=sm["ti"][:, :], in1=sm["t2"][:, :], op=ALU.add))
        small("dve", lambda e: e.tensor_scalar(out=sm["ti"][:, :], in0=sm["ti"][:, :], scalar1=TWO_PI, scalar2=None, op0=ALU.mult))
        small("dve", lambda e: e.tensor_tensor(out=sm["t1"][:, :], in0=sm["ti"][:, :], in1=sm["ti"][:, :], op=ALU.mult))
        horner("t2", "t1", [((-1.0) ** k_) / fact[2 * k_ + 1] for k_ in range(9, -1, -1)])
        small("dve", lambda e: e.tensor_tensor(out=sm[dst][:, :], in0=sm["t2"][:, :], in1=sm["ti"][:, :], op=ALU.mult))

    reduce_sin("u", "sn")
    reduce_sin("uc", "cs")
    small("dve", lambda e: e.tensor_tensor(out=sm["x"][:, :], in0=sm["mag"][:, :], in1=sm["cs"][:, :], op=ALU.mult))
    small("dve", lambda e: e.tensor_scalar(out=sm["x"][:, :], in0=sm["x"][:, :], scalar1=-1.0, scalar2=None, op0=ALU.add))
    small("dve", lambda e: e.tensor_tensor(out=sm["y"][:, :], in0=sm["mag"][:, :], in1=sm["sn"][:, :], op=ALU.mult))
    small("dve", lambda e: e.tensor_tensor(out=sm["den"][:, :], in0=a_re, in1=a_re, op=ALU.mult))
    small("dve", lambda e: e.tensor_tensor(out=sm["t0"][:, :], in0=a_im, in1=a_im, op=ALU.mult))
    small("dve", lambda e: e.tensor_tensor(out=sm["den"][:, :], in0=sm["den"][:, :], in1=sm["t0"][:, :], op=ALU.add))
    small("dve", lambda e: e.reciprocal(out=sm["den"][:, :], in_=sm["den"][:, :]))
    small("dve", lambda e: e.tensor_tensor(out=sm["cr"][:, :], in0=sm["x"][:, :], in1=a_re, op=ALU.mult))
    small("dve", lambda e: e.tensor_tensor(out=sm["t0"][:, :], in0=sm["y"][:, :], in1=a_im, op=ALU.mult))
    small("dve", lambda e: e.tensor_tensor(out=sm["cr"][:, :], in0=sm["cr"][:, :], in1=sm["t0"][:, :], op=ALU.add))
    small("dve", lambda e: e.tensor_tensor(out=sm["cr"][:, :], in0=sm["cr"][:, :], in1=sm["den"][:, :], op=ALU.mult))
    small("dve", lambda e: e.tensor_tensor(out=sm["ci"][:, :], in0=sm["y"][:, :], in1=a_re, op=ALU.mult))
    small("dve", lambda e: e.tensor_tensor(out=sm["t0"][:, :], in0=sm["x"][:, :], in1=a_im, op=ALU.mult))
    small("dve", lambda e: e.tensor_tensor(out=sm["ci"][:, :], in0=sm["ci"][:, :], in1=sm["t0"][:, :], op=ALU.subtract))
    small("dve", lambda e: e.tensor_tensor(out=sm["ci"][:, :], in0=sm["ci"][:, :], in1=sm["den"][:, :], op=ALU.mult))
    yield
    E2 = sb("s5E2", [128, 4, 2, L], F32)
    E1 = sb("s5E1", [128, 4, 2, L], F32)
    dec = sb("s5dec", [128, 4, L], F32)
    wk = sb("s5wk", [128, 4, 2], F32)
    wt = sb("s5wt", [128, 4, 2], F32)
    tl = sb("s5tl", [128, L], F32)
    EL = sb("s5EL", [128, 4, 2], F32)
    kt = ["s5tabs", "s5small"]

    def tb(eng, fn):
        P.emit(eng, fn, reads=kt, writes=["s5tabs"])

    tb("dve", lambda e: e.memset(E2[:, :, 0, 0:1], 1.0))
    tb("dve", lambda e: e.memset(E2[:, :, 1, 0:1], 0.0))
    tb("dve", lambda e: e.tensor_copy(out=wk[:, :, 0], in_=sm["cs"][:, :]))
    tb("dve", lambda e: e.tensor_copy(out=wk[:, :, 1], in_=sm["sn"][:, :]))
    for kk in range(LG):
        n = 1 << kk
        for q in range(4):
            wre, wim = wk[:, q, 0:1], wk[:, q, 1:2]
            ire, iim = E2[:, q, 0, 0:n], E2[:, q, 1, 0:n]
            ore, oim = E2[:, q, 0, n:2 * n], E2[:, q, 1, n:2 * n]
            tb("dve", lambda e, iim=iim, wim=wim, n=n: e.tensor_scalar(out=tl[:, 0:n], in0=iim, scalar1=wim, scalar2=None, op0=ALU.mult))
            tb("dve", lambda e, ire=ire, wre=wre, ore=ore, n=n: e.scalar_tensor_tensor(out=ore, in0=ire, scalar=wre, in1=tl[:, 0:n], op0=ALU.mult, op1=ALU.subtract))
            tb("dve", lambda e, iim=iim, wre=wre, n=n: e.tensor_scalar(out=tl[:, 0:n], in0=iim, scalar1=wre, scalar2=None, op0=ALU.mult))
            tb("dve", lambda e, ire=ire, wim=wim, oim=oim, n=n: e.scalar_tensor_tensor(out=oim, in0=ire, scalar=wim, in1=tl[:, 0:n], op0=ALU.mult, op1=ALU.add))
        tb("dve", lambda e: e.tensor_tensor(out=wt[:, :, 0], in0=wk[:, :, 0], in1=wk[:, :, 0], op=ALU.mult))
        tb("dve", lambda e: e.tensor_tensor(out=wt[:, :, 1], in0=wk[:, :, 1], in1=wk[:, :, 1], op=ALU.mult))
        tb("dve", lambda e: e.tensor_tensor(out=wt[:, :, 1], in0=wt[:, :, 0], in1=wt[:, :, 1], op=ALU.subtract))
        tb("dve", lambda e: e.tensor_tensor(out=wt[:, :, 0], in0=wk[:, :, 0], in1=wk[:, :, 1], op=ALU.mult))
        tb("dve", lambda e: e.tensor_copy(out=wk[:, :, 0], in_=wt[:, :, 1]))
        tb("dve", lambda e: e.tensor_scalar(out=wk[:, :, 1], in0=wt[:, :, 0], scalar1=2.0, scalar2=None, op0=ALU.mult))
    tb("dve", lambda e: e.tensor_copy(out=EL[:, :, :], in_=wk[:, :, :]))
    tb("dve", lambda e: e.memset(tl[:, :], 1.0))
    for q in range(4):
        cr, ci = sm["cr"][:, q:q + 1], sm["ci"][:, q:q + 1]
        tb("pool", lambda e, q=q: e.tensor_scalar(out=dec[:, q, :], in0=tl[:, :], scalar1=sm["mag"][:, q:q + 1], scalar2=None, op0=ALU.mult))
        tb("dve", lambda e, q=q, ci=ci: e.tensor_scalar(out=E1[:, q, 1, :], in0=E2[:, q, 1, :], scalar1=ci, scalar2=None, op0=ALU.mult))
        tb("dve", lambda e, q=q, cr=cr: e.scalar_tensor_tensor(out=E1[:, q, 0, :], in0=E2[:, q, 0, :], scalar=cr, in1=E1[:, q, 1, :], op0=ALU.mult, op1=ALU.add))
        tb("dve", lambda e, q=q, cr=cr: e.tensor_scalar(out=E1[:, q, 1, :], in0=E2[:, q, 1, :], scalar1=cr, scalar2=None, op0=ALU.mult))
        tb("dve", lambda e, q=q, ci=ci: e.scalar_tensor_tensor(out=E1[:, q, 1, :], in0=E2[:, q, 0, :], scalar=ci, in1=E1[:, q, 1, :], op0=ALU.mult, op1=ALU.subtract))
    yield
    NBUF = 2
    t1 = [sb(f"s5t1_{i}", [128, L], F32) for i in range(NBUF)]
    t2 = [sb(f"s5t2_{i}", [128, L], F32) for i in range(NBUF)]
    bre = [sb(f"s5bre{i}", [128, L], F32) for i in range(NBUF)]
    bim = [sb(f"s5bim{i}", [128, L], F32) for i in range(NBUF)]
    vre = [sb(f"s5vre{q}", [128, L], F32) for q in range(4)]
    vim = [sb(f"s5vim{q}", [128, L], F32) for q in range(4)]
    xre = [sb(f"s5xre{i}", [128, L], BF16) for i in range(NBUF)]
    xim = [sb(f"s5xim{i}", [128, L], BF16) for i in range(NBUF)]
    ini = [sb(f"s5ini{q}", [128, 4], F32) for q in range(4)]
    ypre = [sb(f"s5yp{i}", [128, L], F32) for i in range(2)]
    yst = [sb(f"s5ys{i}", [128, L], BF16) for i in range(2)]
    it = 0
    for c in range(NCH):
        csl = slice(c * L, (c + 1) * L)
        ybank = 4 + c % 2
        for q in range(4):
            i = it % NBUF
            it += 1
            ba_, bb_ = (0, 1) if it % 2 == 0 else (2, 3)
            P.emit("pe", lambda e, q=q, ba_=ba_, csl=csl: e.matmul(ps[:, ba_ * 512: ba_ * 512 + L], lhsT=bblk[:, q, 0, :], rhs=uT[:, csl], start=True, stop=True),
                   reads=["s5bb", ("uT", c * L // 512)], writes=[("ps", ba_)])
            P.emit("pe", lambda e, q=q, bb_=bb_, csl=csl: e.matmul(ps[:, bb_ * 512: bb_ * 512 + L], lhsT=bblk[:, q, 1, :], rhs=uT[:, csl], start=True, stop=True),
                   reads=["s5bb", ("uT", c * L // 512)], writes=[("ps", bb_)])
            pre = ps[:, ba_ * 512: ba_ * 512 + L]
            pim = ps[:, bb_ * 512: bb_ * 512 + L]
            P.emit("dve", lambda e, q=q, i=i, pre=pre: e.tensor_tensor(out=t1[i][:, :], in0=pre, in1=E1[:, q, 0, :], op=ALU.mult),
                   reads=[("ps", ba_), "s5tabs"], writes=[("s5t1", i)])
            P.emit("dve", lambda e, q=q, i=i, pim=pim: e.tensor_tensor(out=t2[i][:, :], in0=pim, in1=E1[:, q, 1, :], op=ALU.mult),
                   reads=[("ps", bb_), "s5tabs"], writes=[("s5t2", i)])
            P.emit("pool", lambda e, i=i: e.tensor_tensor(out=bre[i][:, :], in0=t1[i][:, :], in1=t2[i][:, :], op=ALU.subtract),
                   reads=[("s5t1", i), ("s5t2", i)], writes=[("s5bre", i)])
            P.emit("dve", lambda e, q=q, i=i, pim=pim: e.tensor_tensor(out=t1[i][:, :], in0=pim, in1=E1[:, q, 0, :], op=ALU.mult),
                   reads=[("ps", bb_), "s5tabs", ("s5bre", i)], writes=[("s5t1", i)])
            P.emit("dve", lambda e, q=q, i=i, pre=pre: e.tensor_tensor(out=t2[i][:, :], in0=pre, in1=E1[:, q, 1, :], op=ALU.mult),
                   reads=[("ps", ba_), "s5tabs", ("s5bre", i)], writes=[("s5t2", i)])
            P.emit("pool", lambda e, i=i: e.tensor_tensor(out=bim[i][:, :], in0=t1[i][:, :], in1=t2[i][:, :], op=ALU.add),
                   reads=[("s5t1", i), ("s5t2", i)], writes=[("s5bim", i)])
            if c == 0:
                ire_, iim_ = zb[:, 0:1], zb[:, 0:1]
                ikeys = ["s5zb"]
            else:
                elr, eli = EL[:, q, 0:1], EL[:, q, 1:2]
                lr, li = vre[q][:, L - 1:L], vim[q][:, L - 1:L]
                inq = ini[q]
                P.emit("dve", lambda e, inq=inq, li=li, eli=eli: e.tensor_scalar(out=inq[:, 2:3], in0=li, scalar1=eli, scalar2=None, op0=ALU.mult),
                       reads=[("s5v", q), "s5tabs"], writes=[("s5ini", q)])
                P.emit("dve", lambda e, inq=inq, lr=lr, elr=elr: e.scalar_tensor_tensor(out=inq[:, 0:1], in0=lr, scalar=elr, in1=inq[:, 2:3], op0=ALU.mult, op1=ALU.subtract),
                       reads=[("s5v", q), "s5tabs", ("s5ini", q)], writes=[("s5ini", q)])
                P.emit("dve", lambda e, inq=inq, li=li, elr=elr: e.tensor_scalar(out=inq[:, 2:3], in0=li, scalar1=elr, scalar2=None, op0=ALU.mult),
                       reads=[("s5v", q), "s5tabs", ("s5ini", q)], writes=[("s5ini", q)])
                P.emit("dve", lambda e, inq=inq, lr=lr, eli=eli: e.scalar_tensor_tensor(out=inq[:, 1:2], in0=lr, scalar=eli, in1=inq[:, 2:3], op0=ALU.mult, op1=ALU.add),
                       reads=[("s5v", q), "s5tabs", ("s5ini", q)], writes=[("s5ini", q)])
                ire_, iim_ = inq[:, 0:1], inq[:, 1:2]
                ikeys = [("s5ini", q)]
            P.emit("dve", lambda e, q=q, i=i, ire_=ire_: e.tensor_tensor_scan(out=vre[q][:, :], data0=dec[:, q, :], data1=bre[i][:, :], initial=ire_, op0=ALU.mult, op1=ALU.add),
                   reads=ikeys + ["s5tabs", ("s5bre", i), ("s5v", q)], writes=[("s5v", q)])
            P.emit("dve", lambda e, q=q, i=i, iim_=iim_: e.tensor_tensor_scan(out=vim[q][:, :], data0=dec[:, q, :], data1=bim[i][:, :], initial=iim_, op0=ALU.mult, op1=ALU.add),
                   reads=ikeys + ["s5tabs", ("s5bim", i), ("s5v", q)], writes=[("s5v", q)])
            P.emit("pool", lambda e, q=q, i=i: e.tensor_tensor(out=t1[i][:, :], in0=vre[q][:, :], in1=E2[:, q, 0, :], op=ALU.mult),
                   reads=[("s5v", q), "s5tabs", ("s5bim", i)], writes=[("s5t1", i)])
            P.emit("pool", lambda e, q=q, i=i: e.tensor_tensor(out=t2[i][:, :], in0=vim[q][:, :], in1=E2[:, q, 1, :], op=ALU.mult),
                   reads=[("s5v", q), "s5tabs", ("s5bim", i)], writes=[("s5t2", i)])
            P.emit("pool", lambda e, i=i: e.tensor_tensor(out=xre[i][:, :], in0=t1[i][:, :], in1=t2[i][:, :], op=ALU.subtract),
                   reads=[("s5t1", i), ("s5t2", i)], writes=[("s5xre", i)])
            P.emit("dve", lambda e, q=q, i=i: e.tensor_tensor(out=bre[i][:, :], in0=vim[q][:, :], in1=E2[:, q, 0, :], op=ALU.mult),
                   reads=[("s5v", q), "s5tabs"], writes=[("s5bre", i)])
            P.emit("pool", lambda e, q=q, i=i: e.tensor_tensor(out=bim[i][:, :], in0=vre[q][:, :], in1=E2[:, q, 1, :], op=ALU.mult),
                   reads=[("s5v", q), "s5tabs"], writes=[("s5bim", i)])
            P.emit("pool", lambda e, i=i: e.tensor_tensor(out=xim[i][:, :], in0=bre[i][:, :], in1=bim[i][:, :], op=ALU.add),
                   reads=[("s5bre", i), ("s5bim", i)], writes=[("s5xim", i)])

            def ymm(e, q=q, i=i, ybank=ybank):
                e.matmul(ps[:, ybank * 512: ybank * 512 + L], lhsT=cblk[:, q, 0, :], rhs=xre[i][:, :], start=(q == 0), stop=False)
                return e.matmul(ps[:, ybank * 512: ybank * 512 + L], lhsT=cblk[:, q, 1, :], rhs=xim[i][:, :], start=False, stop=(q == 3))

            P.emit("pe", ymm, reads=["s5cb", ("s5xre", i), ("s5xim", i)], writes=[("ps", ybank)])
            yield
        j = c % 2
        P.emit("dve", lambda e, j=j, ybank=ybank, csl=csl: e.scalar_tensor_tensor(out=ypre[j][:, :], in0=uT[:, csl], scalar=dcol[:, 0:1], in1=ps[:, ybank * 512: ybank * 512 + L],
                                                                                  op0=ALU.mult, op1=ALU.add),
               reads=[("uT", c * L // 512), "s5dc", ("ps", ybank)], writes=[("s5yp", j)])
        P.emit("act", lambda e, j=j: e.activation(out=yst[j][:, :], in_=ypre[j][:, :], func=AF.Gelu), reads=[("s5yp", j)], writes=[("s5ys", j)])
        out_toks.append(P.dma("sp", lambda e, j=j, csl=csl: e.dma_start(out=yT_out[384:512, csl], in_=yst[j][:, :]), reads=[("s5ys", j)], writes=[("yout_s", c)]))
        yield


D_MODEL, D_FF, SEQ, BATCH, DEPTH = 2048, 5632, 4096, 2, 4
NTOK = 1024


def _new_ctx(nc, stack):
    C = Ctx()
    C.nc, C.D, C.F, C.NT, C.KC = nc, D_MODEL, D_FF, NTOK, D_MODEL // 128
    P = Prog(nc, stack)
    ps = stack.enter_context(nc.psum_tensor("ps", [128, 4096], F32))
    epsb = stack.enter_context(nc.sbuf_tensor("sb_epsb", [128, 1], F32))
    P.emit("dve", lambda e: e.memset(epsb[:, :], EPS), reads=[], writes=["epsb"])
    C.ps = ps
    C.xs = {"ps": ps, "psb": ps.bitcast(BF16), "epsb": epsb}
    C.tr_bank = 6
    return C, P


def _ident_tile(C, P, stack, ident_d):
    t = stack.enter_context(C.nc.sbuf_tensor("sb_identb", [128, 128], BF16))
    P.dma("pool", lambda e: e.dma_start(out=t[:, :], in_=ident_d), reads=[], writes=["ident"])
    return t


def _dr(nc, name, shape, dt, kind="ExternalInput"):
    return nc.dram_tensor(name, shape, dt, kind=kind).ap()


def build_pre():
    nc = bass.Bass("TRN2", target_bir_lowering=False)
    x_in = _dr(nc, "x_in", [NTOK, D_MODEL], F32)
    gnext = _dr(nc, "gnext", [D_MODEL], F32)
    ident = _dr(nc, "ident", [128, 128], F32)
    hT_out = _dr(nc, "hT_out", [D_MODEL, NTOK], BF16, "ExternalOutput")
    with ExitStack() as stack:
        C, P = _new_ctx(nc, stack)
        idt = _ident_tile(C, P, stack, ident)
        toks = pre_phase(C, P, x_in, hT_out, gnext, idt)
        P.finish(toks)
        P.replay()
    return nc


def build_ffn():
    nc = bass.Bass("TRN2", target_bir_lowering=False)
    KC, FC, DC, FG = D_MODEL // 128, D_FF // 128, D_MODEL // 512, D_FF // 512
    x_in = _dr(nc, "x_in", [NTOK, D_MODEL], F32)
    hT_in = _dr(nc, "hT_in", [D_MODEL, NTOK], BF16)
    wg_t = _dr(nc, "wg_t", [FC, 128, KC, 128], F32)
    wu_t = _dr(nc, "wu_t", [FC, 128, KC, 128], F32)
    wd_t = _dr(nc, "wd_t", [DC, FG, 128, 4, 512], F32)
    gpost = _dr(nc, "gpost", [D_MODEL], F32)
    gnext = _dr(nc, "gnext", [D_MODEL], F32)
    ident = _dr(nc, "ident", [128, 128], F32)
    x_out = _dr(nc, "x_out", [NTOK, D_MODEL], F32, "ExternalOutput")
    hT_out = _dr(nc, "hT_out", [D_MODEL, NTOK], BF16, "ExternalOutput")
    ot_d = _dr(nc, "ot_d", [NTOK, D_MODEL], F32, "Internal")
    with ExitStack() as stack:
        C, P = _new_ctx(nc, stack)
        idt = _ident_tile(C, P, stack, ident)
        toks = ffn_phase(C, P, x_in, x_out, hT_in, hT_out, wg_t, wu_t, wd_t, gpost, gnext, ot_d, idt)
        P.finish(toks)
        P.replay()
    return nc


def build_out():
    nc = bass.Bass("TRN2", target_bir_lowering=False)
    x_in = _dr(nc, "x_in", [NTOK, D_MODEL], F32)
    yT_in = _dr(nc, "yT_in", [D_MODEL, NTOK], BF16)
    wo_t = _dr(nc, "wo_t", [4, 4, 128, 4, 512], F32)
    gluw = _dr(nc, "gluw", [128, 4, 512], F32)
    glub = _dr(nc, "glub", [128, 4], F32)
    gpost = _dr(nc, "gpost", [D_MODEL], F32)
    gnext = _dr(nc, "gnext", [D_MODEL], F32)
    ident = _dr(nc, "ident", [128, 128], F32)
    x_out = _dr(nc, "x_out", [NTOK, D_MODEL], F32, "ExternalOutput")
    hT_out = _dr(nc, "hT_out", [D_MODEL, NTOK], BF16, "ExternalOutput")
    ot_d = _dr(nc, "ot_d", [NTOK, D_MODEL], F32, "Internal")
    with ExitStack() as stack:
        C, P = _new_ctx(nc, stack)
        idt = _ident_tile(C, P, stack, ident)
        toks = out_phase(C, P, x_in, x_out, yT_in, hT_out, wo_t, gluw, glub, gpost, gnext, ot_d, idt)
        P.finish(toks)
        P.replay()
    return nc


def build_mix():
    nc = bass.Bass("TRN2", target_bir_lowering=False)
    hT = _dr(nc, "hT", [D_MODEL, S], BF16)
    win_fm = _dr(nc, "win_fm", [128, 16, 1024], F32)
    win_tm = _dr(nc, "win_tm", [128, 16, 256], F32)
    consts = _dr(nc, "consts", [128, NCONST], F32)
    ropecos = _dr(nc, "ropecos", [128, S], F32)
    ropesin = _dr(nc, "ropesin", [128, S], F32)
    sinks = _dr(nc, "sinks", [2], F32)
    convw = _dr(nc, "convw", [128, 3, 4], F32)
    dnp = _dr(nc, "dnp", [2], F32)
    dnnw = _dr(nc, "dnnw", [128, 1], F32)
    s5tab = _dr(nc, "s5tab", [128, 4, 3], F32)
    s5b = _dr(nc, "s5b", [128, 4, 2, 128], F32)
    s5c = _dr(nc, "s5c", [128, 4, 2, 128], F32)
    s5d = _dr(nc, "s5d", [128, 1], F32)
    yT = _dr(nc, "yT", [512, S], BF16, "ExternalOutput")
    with ExitStack() as stack:
        C, P = _new_ctx(nc, stack)
        toks, top = mixer_phase(C, P, stack, hT, win_fm, win_tm, consts, ropecos, ropesin, sinks, convw, dnp, dnnw, s5tab, s5b, s5c, s5d, yT)
        P.finish(toks)
        P.replay()
        top.close()
    return nc


def host_mixer_inputs(j, w_in, sinks, conv_w, a_log, dt_bias, norm_w, s5):
    KC = 16
    cols = []
    cols += list(range(256 * j, 256 * j + 256))
    cols += list(range(1024 + 128 * (j // 2), 1024 + 128 * (j // 2) + 128))
    for c in range(3):
        cols += list(range(1536 + 512 * c + 128 * j, 1536 + 512 * c + 128 * j + 128))
    cols += list(range(3072 + 128 * j, 3072 + 128 * j + 128))
    cols += list(range(3592 + 128 * j, 3592 + 128 * j + 128))
    fm = w_in[:, cols]
    win_fm = np.ascontiguousarray(fm.reshape(KC, 128, 1024).transpose(1, 0, 2))
    tcols = list(range(1280 + 128 * (j // 2), 1280 + 128 * (j // 2) + 128)) + [3584 + j, 3588 + j]
    tm = w_in[:, tcols]
    win_tm = np.zeros((128, KC, 256), np.float32)
    win_tm[:, :, 0:130] = tm.reshape(KC, 128, 130).transpose(1, 0, 2)
    convw = np.stack([conv_w[:, 512 * c + 128 * j: 512 * c + 128 * j + 128].T for c in range(3)], 1)
    dnp = np.array([a_log[j], dt_bias[j]], np.float32)
    dnnw = norm_w.reshape(128, 1)
    sk = sinks[2 * j: 2 * j + 2]
    a_re, a_im, log_dt, b_re, b_im, c_re, c_im, d_skip = s5
    g0 = 8 * j
    s5tab = np.zeros((128, 4, 3), np.float32)
    s5b = np.zeros((128, 4, 2, 128), np.float32)
    s5c = np.zeros((128, 4, 2, 128), np.float32)
    for q in range(4):
        for g2 in range(2):
            gl = 2 * q + g2
            g = g0 + gl
            s5tab[g2 * 64:(g2 + 1) * 64, q, 0] = a_re[g]
            s5tab[g2 * 64:(g2 + 1) * 64, q, 1] = a_im[g]
            s5tab[g2 * 64:(g2 + 1) * 64, q, 2] = log_dt[g]
            s5b[gl * 16:(gl + 1) * 16, q, 0, g2 * 64:(g2 + 1) * 64] = b_re[g].T
            s5b[gl * 16:(gl + 1) * 16, q, 1, g2 * 64:(g2 + 1) * 64] = b_im[g].T
            s5c[g2 * 64:(g2 + 1) * 64, q, 0, gl * 16:(gl + 1) * 16] = c_re[g].T
            s5c[g2 * 64:(g2 + 1) * 64, q, 1, gl * 16:(gl + 1) * 16] = c_im[g].T
    s5d = d_skip[128 * j: 128 * j + 128].reshape(128, 1)
    return dict(win_fm=win_fm, win_tm=win_tm, convw=np.ascontiguousarray(convw), dnp=dnp, dnnw=np.ascontiguousarray(dnnw),
                sinks=np.ascontiguousarray(sk), s5tab=s5tab, s5b=s5b, s5c=s5c, s5d=np.ascontiguousarray(s5d))


def _ffn_layout(Wg, Wu, Wd):
    KC, FC, DC, FG = D_MODEL // 128, D_FF // 128, D_MODEL // 512, D_FF // 512
    wg_t = np.ascontiguousarray(Wg.reshape(KC, 128, FC, 128).transpose(2, 1, 0, 3))
    wu_t = np.ascontiguousarray(Wu.reshape(KC, 128, FC, 128).transpose(2, 1, 0, 3))
    wd_t = np.ascontiguousarray(Wd.reshape(FG, 4, 128, DC, 512).transpose(3, 0, 2, 1, 4))
    return wg_t, wu_t, wd_t


_PROGS = {}


def _prog(name):
    if name not in _PROGS:
        _PROGS[name] = {"pre": build_pre, "ffn": build_ffn, "mix": build_mix, "out": build_out}[name]()
    return _PROGS[name]


def _run(name, in_maps):
    res = run_bass_kernel_spmd(_prog(name), in_maps, core_ids=list(range(8)))
    return res.results


def kernel(**inp):
    f = lambda k: np.asarray(inp[k], dtype=np.float32)
    x = f("x")
    ident = np.eye(128, dtype=np.float32)
    consts = make_consts()
    rc, rs = rope_tables()
    xs = [np.ascontiguousarray(x[c // 4, (c % 4) * NTOK:(c % 4 + 1) * NTOK]) for c in range(8)]
    r = _run("pre", [dict(x_in=xs[c], gnext=f("ff1_norm_pre")[0], ident=ident) for c in range(8)])
    hTs = [np.asarray(r[c]["hT_out"]) for c in range(8)]
    for l in range(DEPTH):
        wg_t, wu_t, wd_t = _ffn_layout(f("ff1_w_gate")[l], f("ff1_w_up")[l], f("ff1_w_down")[l])
        r = _run("ffn", [dict(x_in=xs[c], hT_in=hTs[c], wg_t=wg_t, wu_t=wu_t, wd_t=wd_t, gpost=f("ff1_norm_post")[l],
                              gnext=f("mix_norm_pre")[l], ident=ident) for c in range(8)])
        xs = [np.asarray(r[c]["x_out"]) for c in range(8)]
        hTs = [np.asarray(r[c]["hT_out"]) for c in range(8)]
        del wg_t, wu_t, wd_t
        hfull = [np.ascontiguousarray(np.concatenate(hTs[4 * b:4 * b + 4], axis=1)) for b in range(BATCH)]
        s5 = tuple(f(k)[l] for k in ("s5_a_re", "s5_a_im", "s5_log_dt", "s5_b_re", "s5_b_im", "s5_c_re", "s5_c_im", "s5_d"))
        hm = [host_mixer_inputs(j, f("w_in")[l], f("attn_sinks")[l], f("dn_conv_w")[l], f("dn_a_log")[l], f("dn_dt_bias")[l],
                                f("dn_norm_w")[l], s5) for j in range(4)]
        r = _run("mix", [dict(hT=hfull[c // 4], consts=consts, ropecos=rc, ropesin=rs, **hm[c % 4]) for c in range(8)])
        ymy = [np.asarray(r[c]["yT"]) for c in range(8)]
        yslices = []
        for b in range(BATCH):
            ynat = np.empty((D_MODEL, SEQ), dtype=ymy[0].dtype)
            for j in range(4):
                y = ymy[4 * b + j]
                ynat[256 * j:256 * j + 256] = y[0:256]
                ynat[1024 + 128 * j:1024 + 128 * j + 128] = y[256:384]
                ynat[1536 + 128 * j:1536 + 128 * j + 128] = y[384:512]
            for t in range(4):
                yslices.append(np.ascontiguousarray(ynat[:, t * NTOK:(t + 1) * NTOK]))
        Wo = f("w_out")[l]
        wo_t = np.ascontiguousarray(Wo.reshape(4, 4, 128, 4, 512).transpose(3, 0, 2, 1, 4))
        gluw = np.ascontiguousarray(f("s5_glu_w")[l].reshape(4, 128, 512).transpose(1, 0, 2))
        glub = np.ascontiguousarray(f("s5_glu_b")[l].reshape(4, 128).T)
        r = _run("out", [dict(x_in=xs[c], yT_in=yslices[c], wo_t=wo_t, gluw=gluw, glub=glub, gpost=f("mix_norm_post")[l],
                              gnext=f("ff2_norm_pre")[l], ident=ident) for c in range(8)])
        xs = [np.asarray(r[c]["x_out"]) for c in range(8)]
        hTs = [np.asarray(r[c]["hT_out"]) for c in range(8)]
        wg_t, wu_t, wd_t = _ffn_layout(f("ff2_w_gate")[l], f("ff2_w_up")[l], f("ff2_w_down")[l])
        gn = f("ff1_norm_pre")[(l + 1) % DEPTH]
        r = _run("ffn", [dict(x_in=xs[c], hT_in=hTs[c], wg_t=wg_t, wu_t=wu_t, wd_t=wd_t, gpost=f("ff2_norm_post")[l],
                              gnext=gn, ident=ident) for c in range(8)])
        xs = [np.asarray(r[c]["x_out"]) for c in range(8)]
        hTs = [np.asarray(r[c]["hT_out"]) for c in range(8)]
        del wg_t, wu_t, wd_t
    out = np.empty((BATCH, SEQ, D_MODEL), np.float32)
    for c in range(8):
        out[c // 4, (c % 4) * NTOK:(c % 4 + 1) * NTOK] = xs[c]
    return out
```
